# Optimizing a Trainium2 kernel written in Bass

```python
import jax, jax.numpy as jnp
from jax import lax
import numpy as np

D_MODEL = 2048
BATCH = 4
SEQ = 2048
DEPTH = 2
DEC_BATCH = 8
DEC_SEQ = 32
PAST_LEN = 1024

CHUNK = 64
N_MIXERS = 2
N_A_LAYERS = (DEPTH + 1) // 2
N_B_LAYERS = DEPTH // 2
EXPAND = 2
GM_WIDTH = EXPAND * D_MODEL
GM_BLOCK = 128
GM_GROUPS = 16
GM_GROUP_DIM = GM_WIDTH // GM_GROUPS
SB_HEADS = 16
SB_HEAD_DIM = D_MODEL // SB_HEADS
SB_WIDTH = SB_HEADS * SB_HEAD_DIM
SB_Q_BLOCK = 128
NORM_EPS = 1e-6
LN_EPS = 1e-5

kernel_name = "stickbreak_gmlp_hybrid_stream_step"


def rms_norm(x, g):
    xf = x.astype(jnp.float32)
    y = xf * lax.rsqrt(jnp.mean(xf * xf, axis=-1, keepdims=True) + NORM_EPS)
    return (y * g.astype(jnp.float32)).astype(x.dtype)


def layer_norm(x, g, b):
    xf = x.astype(jnp.float32)
    mu = jnp.mean(xf, axis=-1, keepdims=True)
    xc = xf - mu
    var = jnp.mean(xc * xc, axis=-1, keepdims=True)
    y = xc * lax.rsqrt(var + LN_EPS) * g.astype(jnp.float32) + b.astype(jnp.float32)
    return y.astype(x.dtype)


def chunk_causal_mask(n):
    pos = jnp.arange(n)
    return (pos[None, :] // CHUNK) <= (pos[:, None] // CHUNK)


def gmlp_branch(h, w_in, ln_g, ln_b, w_s, b_s, w_out):
    bsz, seq_len, _ = h.shape
    blk = min(seq_len, GM_BLOCK)
    n_blk = seq_len // blk
    proj = jnp.einsum('bld,de->ble', h, w_in)
    u, v, z = jnp.split(proj, 3, axis=-1)
    u = jax.nn.gelu(u)
    v = layer_norm(jax.nn.gelu(v), ln_g, ln_b)
    w = w_s[:, :blk, :blk] * chunk_causal_mask(blk).astype(w_s.dtype)
    vb = v.reshape(bsz, n_blk, blk, GM_GROUPS, GM_GROUP_DIM)
    mixed = jnp.einsum('gts,bnsgc->bntgc', w, vb) + b_s[:, :blk].T[None, None, :, :, None]
    s = u * mixed.reshape(bsz, seq_len, GM_WIDTH)
    y = s * jax.nn.silu(z)
    return jnp.einsum('ble,ed->bld', y, w_out), v


def stick_breaking_attend(q, k, v, q_offset):
    tq = q.shape[1]
    scale = SB_HEAD_DIM ** -0.5
    outs = []
    for start in range(0, tq, SB_Q_BLOCK):
        stop = min(start + SB_Q_BLOCK, tq)
        n_keys = q_offset + stop
        qb = q[:, start:stop].astype(jnp.float32)
        kb = k[:, :n_keys].astype(jnp.float32)
        vb = v[:, :n_keys].astype(jnp.float32)
        logits = jnp.einsum('bqhd,bkhd->bhqk', qb, kb) * scale
        t_pos = q_offset + jnp.arange(start, stop)
        s_pos = jnp.arange(n_keys)
        strict = s_pos[None, :] < t_pos[:, None]
        log_fail = jnp.where(strict, jax.nn.log_sigmoid(-logits), 0.0)
        later_fail = lax.cumsum(log_fail, axis=3, reverse=True) - log_fail
        weights = jnp.where(strict, jnp.exp(jax.nn.log_sigmoid(logits) + later_fail), 0.0)
        outs.append(jnp.einsum('bhqk,bkhd->bqhd', weights, vb))
    return jnp.concatenate(outs, axis=1).astype(v.dtype)


def sb_branch(h, w_in, w_out, cache_k, cache_v):
    bsz, seq_len, _ = h.shape
    proj = jnp.einsum('bld,de->ble', h, w_in)
    q, k, v, z = jnp.split(proj, 4, axis=-1)
    q = q.reshape(bsz, seq_len, SB_HEADS, SB_HEAD_DIM)
    k = k.reshape(bsz, seq_len, SB_HEADS, SB_HEAD_DIM)
    v = v.reshape(bsz, seq_len, SB_HEADS, SB_HEAD_DIM)
    if cache_k is None:
        o = stick_breaking_attend(q, k, v, 0)
    else:
        k_all = jnp.concatenate([cache_k.astype(k.dtype), k], axis=1)
        v_all = jnp.concatenate([cache_v.astype(v.dtype), v], axis=1)
        o = stick_breaking_attend(q, k_all, v_all, cache_k.shape[1])
    y = o.reshape(bsz, seq_len, SB_WIDTH) * jax.nn.silu(z)
    return jnp.einsum('ble,ed->bld', y, w_out), k, v


def setup_inputs(seed: int = 0) -> dict:
    key = jax.random.key(seed)
    ks = jax.random.split(key, 16)
    f32 = jnp.float32
    x_prompt = jax.random.normal(ks[0], (BATCH, SEQ, D_MODEL), f32)
    x_sample = jax.random.normal(ks[1], (DEC_BATCH, DEC_SEQ, D_MODEL), f32)
    cache_sb_k = jax.random.normal(ks[2], (N_B_LAYERS, DEC_BATCH, PAST_LEN, SB_HEADS, SB_HEAD_DIM), f32)
    cache_sb_v = jax.random.normal(ks[3], (N_B_LAYERS, DEC_BATCH, PAST_LEN, SB_HEADS, SB_HEAD_DIM), f32)
    norm_g = 1.0 + 0.02 * jax.random.normal(ks[4], (DEPTH, D_MODEL), f32)
    final_norm_g = 1.0 + 0.02 * jax.random.normal(ks[5], (D_MODEL,), f32)
    gm_w_in = jax.random.normal(ks[6], (N_A_LAYERS, D_MODEL, 3 * GM_WIDTH), f32) * D_MODEL ** -0.5
    gm_ln_g = 1.0 + 0.02 * jax.random.normal(ks[7], (N_A_LAYERS, GM_WIDTH), f32)
    gm_ln_b = 0.02 * jax.random.normal(ks[8], (N_A_LAYERS, GM_WIDTH), f32)
    gm_w_s = jax.random.normal(ks[9], (N_A_LAYERS, GM_GROUPS, GM_BLOCK, GM_BLOCK), f32) * GM_BLOCK ** -0.5
    gm_b_s = 1.0 + 0.02 * jax.random.normal(ks[10], (N_A_LAYERS, GM_GROUPS, GM_BLOCK), f32)
    gm_w_out = jax.random.normal(ks[11], (N_A_LAYERS, GM_WIDTH, D_MODEL), f32) * GM_WIDTH ** -0.5
    sb_w_in = jax.random.normal(ks[12], (N_B_LAYERS, D_MODEL, 4 * SB_WIDTH), f32) * D_MODEL ** -0.5
    sb_w_out = jax.random.normal(ks[13], (N_B_LAYERS, SB_WIDTH, D_MODEL), f32) * SB_WIDTH ** -0.5
    return {"x_prompt": x_prompt, "x_sample": x_sample, "cache_sb_k": cache_sb_k, "cache_sb_v": cache_sb_v,
            "norm_g": norm_g, "final_norm_g": final_norm_g,
            "gm_w_in": gm_w_in, "gm_ln_g": gm_ln_g, "gm_ln_b": gm_ln_b, "gm_w_s": gm_w_s, "gm_b_s": gm_b_s,
            "gm_w_out": gm_w_out, "sb_w_in": sb_w_in, "sb_w_out": sb_w_out}


def reference(x_prompt, x_sample, cache_sb_k, cache_sb_v, norm_g, final_norm_g,
              gm_w_in, gm_ln_g, gm_ln_b, gm_w_s, gm_b_s, gm_w_out, sb_w_in, sb_w_out):
    xp, xs = x_prompt, x_sample
    gm_v_rows = []
    kp_rows, vp_rows, ks_rows, vs_rows = [], [], [], []
    for i in range(DEPTH):
        hp = rms_norm(xp, norm_g[i])
        hs = rms_norm(xs, norm_g[i])
        j = i // N_MIXERS
        if i % N_MIXERS == 0:
            dp, _ = gmlp_branch(hp, gm_w_in[j], gm_ln_g[j], gm_ln_b[j], gm_w_s[j], gm_b_s[j], gm_w_out[j])
            ds, v_new = gmlp_branch(hs, gm_w_in[j], gm_ln_g[j], gm_ln_b[j], gm_w_s[j], gm_b_s[j], gm_w_out[j])
            gm_v_rows.append(v_new)
        else:
            dp, kp, vp = sb_branch(hp, sb_w_in[j], sb_w_out[j], None, None)
            ds, kn, vn = sb_branch(hs, sb_w_in[j], sb_w_out[j], cache_sb_k[j], cache_sb_v[j])
            kp_rows.append(kp)
            vp_rows.append(vp)
            ks_rows.append(kn)
            vs_rows.append(vn)
        xp = xp + dp
        xs = xs + ds
    y_prompt = rms_norm(xp, final_norm_g)
    y_sample = rms_norm(xs, final_norm_g)
    k_prompt_new = jnp.stack(kp_rows)
    v_prompt_new = jnp.stack(vp_rows)
    k_sample_new = jnp.stack(ks_rows)
    v_sample_new = jnp.stack(vs_rows)
    gm_v_sample = jnp.stack(gm_v_rows)
    return (y_prompt, y_sample, k_prompt_new, v_prompt_new, k_sample_new, v_sample_new, gm_v_sample)
```

```python
import numpy as np
import concourse.bass as bass
import concourse.mybir as mybir
from concourse.bass_utils import run_bass_kernel_spmd

F32 = mybir.dt.float32
BF16 = mybir.dt.bfloat16
F16 = mybir.dt.float16
AF = mybir.ActivationFunctionType
ALU = mybir.AluOpType
AX = mybir.AxisListType

D = 2048
T = 1056
NT = 9
PT = [128] * 8 + [32]
GW = 4096
NCORES = 8
G_BLOCKS = ([0, 3, 4, 7, 8, 11, 12, 15], [1, 2, 5, 6, 9, 10, 13, 14])
SCALE = 128.0 ** -0.5

OFF_CONST = 0
CONST_SZ = 6784
OFF_VN = OFF_CONST + CONST_SZ
VN_SZ = 73728
OFF_BIG = OFF_VN + VN_SZ
BIG_SZ = 99328
OFF_TEMP = OFF_BIG + BIG_SZ
TEMP_SZ = 32768
ARENA_BYTES = OFF_TEMP + TEMP_SZ
OFF_HI = OFF_BIG + 49152


WAR_SAME = True
class Buf:
    __slots__ = ("name", "w", "r", "psum")

    def __init__(self, name, init=None, psum=False):
        self.name = name
        self.w = None
        self.r = dict(init) if init else {}
        self.psum = psum

    def alldeps(self):
        d = dict(self.r)
        if self.w is not None:
            k, v = self.w
            if d.get(k, 0) < v:
                d[k] = v
        return d


class Region:
    def __init__(self, name):
        self.name = name
        self.hist = {}
        self.live = []

    def flip(self):
        for b in self.live:
            for k, v in b.alldeps().items():
                if self.hist.get(k, 0) < v:
                    self.hist[k] = v
        self.live = []

    def new(self, name):
        b = Buf(name, self.hist)
        self.live.append(b)
        return b


class Sched:
    ENGS = ("pe", "act", "dve", "pool", "sp")

    def __init__(self, nc):
        self.nc = nc
        self.streams = {e: [] for e in self.ENGS}
        self.sem = {}
        self.cnt = {}
        self.waited = {e: {} for e in self.ENGS}
        for e in self.ENGS:
            self._mksem("tl_" + e)
        self.out_deps = []

    def _mksem(self, key):
        h = self.nc.alloc_semaphore(name=key)
        self.sem[key] = h
        self.cnt[key] = 0
        return h

    def _deps(self, eng, reads, writes):
        deps = {}
        own = "tl_" + eng

        def add(k, v, war=False):
            if k == own and (eng == "pe" or (war and not WAR_SAME)):
                return
            if deps.get(k, 0) < v:
                deps[k] = v
        for b in reads:
            if b.psum:
                if b.w is not None and b.w[0] != own:
                    add(b.w[0], b.w[1])
                for k, v in b.r.items():
                    if k != own:
                        add(k, v)
            elif b.w is not None:
                add(*b.w)
        for b in writes:
            if b.w is not None and not (b.psum and b.w[0] == own):
                add(b.w[0], b.w[1])
            for k, v in b.r.items():
                if not (b.psum and k == own):
                    add(k, v, war=True)
        out = []
        wd = self.waited[eng]
        for k, v in deps.items():
            if wd.get(k, 0) < v:
                wd[k] = v
                out.append((k, v))
        return out

    def _commit(self, dep, reads, writes):
        for b in writes:
            b.w = dep
            b.r = {}
        k, v = dep
        for b in reads:
            if b not in writes:
                if b.psum:
                    b.w = dep
                    b.r = {}
                elif b.r.get(k, 0) < v:
                    b.r[k] = v

    def op(self, eng, fn, reads=(), writes=()):
        waits = self._deps(eng, reads, writes)
        key = "tl_" + eng
        self.cnt[key] += 1
        dep = (key, self.cnt[key])
        sems = self.sem

        def item(e, waits=waits, fn=fn, key=key):
            for (k, v) in waits:
                e.wait_ge(sems[k], v)
            ins = fn(e)
            ins.then_inc(sems[key], 1)
        self.streams[eng].append(item)
        self._commit(dep, reads, writes)
        return dep

    def dma(self, q, semkey, out, in_, reads=(), writes=(), final=False):
        if semkey not in self.sem:
            self._mksem(semkey)
        waits = self._deps(q, reads, writes)
        self.cnt[semkey] += 16
        dep = (semkey, self.cnt[semkey])
        sems = self.sem

        def item(e, waits=waits, out=out, in_=in_, semkey=semkey):
            for (k, v) in waits:
                e.wait_ge(sems[k], v)
            e.dma_start(out=out, in_=in_).then_inc(sems[semkey], 16)
        self.streams[q].append(item)
        self._commit(dep, reads, writes)
        if final:
            self.out_deps.append(dep)
        return dep

    def custom(self, eng, semkey, inc, fn, reads=(), writes=()):
        if semkey not in self.sem:
            self._mksem(semkey)
        waits = self._deps(eng, reads, writes)
        self.cnt[semkey] += inc
        dep = (semkey, self.cnt[semkey])
        sems = self.sem

        def item(e, waits=waits, fn=fn, semkey=semkey, inc=inc):
            for (k, v) in waits:
                e.wait_ge(sems[k], v)
            fn(e).then_inc(sems[semkey], inc)
        self.streams[eng].append(item)
        self._commit(dep, reads, writes)
        return dep

    def build(self):
        nc = self.nc
        fin = {}
        for (k, v) in self.out_deps:
            if fin.get(k, 0) < v:
                fin[k] = v
        sems = self.sem

        def fin_item(e):
            for k, v in fin.items():
                e.wait_ge(sems[k], v)
        self.streams["sp"].append(fin_item)
        streams = self.streams
        with nc.Block() as block:
            @block.tensor
            def _(e):
                for it in streams["pe"]:
                    it(e)

            @block.scalar
            def _(e):
                for it in streams["act"]:
                    it(e)

            @block.vector
            def _(e):
                for it in streams["dve"]:
                    it(e)

            @block.gpsimd
            def _(e):
                for it in streams["pool"]:
                    it(e)

            @block.sync
            def _(e):
                for it in streams["sp"]:
                    it(e)


def build_program(stage=2):
    nc = bass.Bass("TRN2", target_bir_lowering=False)

    def din(name, shape, dt=F32):
        return nc.dram_tensor(name, list(shape), dt, kind="ExternalInput").ap()

    def dout(name, shape, dt=F32):
        return nc.dram_tensor(name, list(shape), dt, kind="ExternalOutput").ap()

    xp = din("xp", [1024, D])
    xs = din("xs", [32, D])
    ck = din("ck", [1024, D])
    cv = din("cv", [1024, D])
    masks_in = din("masks", [5, 128, 128])
    norm_g = din("norm_g", [2, D])
    final_g = din("final_g", [1, D])
    gm_w_in = din("gm_w_in", [D, 3 * GW])
    gm_ln_g = din("gm_ln_g", [1, GW])
    gm_ln_b = din("gm_ln_b", [1, GW])
    gm_w_s = din("gm_w_s", [16 * 128, 128])
    gm_b_s = din("gm_b_s", [1, 16 * 128])
    gm_w_out = din("gm_w_out", [GW, D])
    sb_w_in = din("sb_w_in", [D, 4 * D])
    sb_w_out = din("sb_w_out", [D, D])

    yp_o = dout("yp", [1024, D])
    ys_o = dout("ys", [32, D])
    kp_o = dout("kp", [1024, D])
    vp_o = dout("vp", [1024, D])
    ks_o = dout("ks", [32, D])
    vs_o = dout("vs", [32, D])
    gmv_o = dout("gmv", [32, GW])

    x1_scr = nc.dram_tensor("x1_scr", [T, D], F32).ap()
    kt_send = [nc.dram_tensor("kt_send%d" % i, [512, 1024], BF16) for i in range(4)]
    kt_all = [nc.dram_tensor("kt_all%d" % i, [1024, 1024], BF16) for i in range(4)]
    v_send = [nc.dram_tensor("v_send%d" % i, [1024, 512], BF16) for i in range(4)]
    v_all = [nc.dram_tensor("v_all%d" % i, [2048, 512], BF16) for i in range(4)]

    arena = nc.alloc_sbuf_tensor("arena", [128, ARENA_BYTES // 2], BF16)
    pall = nc.alloc_psum_tensor("pall", [128, 4096], F32)

    def view(off, nbytes, dt=BF16, pat=None, **kw):
        assert off % 4 == 0 and off + nbytes <= ARENA_BYTES, (off, nbytes)
        a = arena[:, off // 2:(off + nbytes) // 2]
        if dt != BF16:
            a = a.bitcast(dt)
        if pat:
            a = a.rearrange(pat, **kw)
        return a

    def bank(b, n=1):
        return pall[:, b * 512:(b + n) * 512]

    S = Sched(nc)
    PSB = [Buf("ps%d" % i, psum=True) for i in range(8)]

    ident = view(0, 256)
    negU = view(256, 256)
    negOnes = view(512, 256)
    ones_row = view(768, 512, F32)
    zero_row = view(1280, 1024)
    masks = view(2304, 2560, F32, "p (m t) -> p m t", m=5)
    ssq = view(4864, 256, F32)
    rsd = view(5120, 256, F32)
    s1 = view(5376, 288, F32, "p (t w) -> p t w", t=9)
    s2 = view(5664, 288, F32, "p (t w) -> p t w", t=9)
    mv = view(5952, 288, F32, "p (t w) -> p t w", t=9)
    fss = view(6240, 144, F32, "p (t w) -> p t w", t=9)
    s12 = view(5376, 576, F32, "p (k t w) -> p k t w", k=2, t=9)
    lnscr = view(6400, 64, F32)
    Blnscr = Buf("lnscr")
    Bconst = Buf("const")
    Bstat = [Buf("stat%d" % i) for i in range(NT)]

    R_temp = Region("temp")
    R_hi = Region("hi")
    R_vn = Region("vn")

    tmpf = view(OFF_TEMP, 512, F32)
    Btmpf = R_temp.new("tmpf")
    S.op("dve", lambda e: e.memset(tmpf, 1.0), writes=[Btmpf])
    S.op("pool", lambda e: e.affine_select(out=tmpf, in_=tmpf, pattern=[[-1, 128]], compare_op=ALU.is_equal,
                                           fill=0.0, base=0, channel_multiplier=1), reads=[Btmpf], writes=[Btmpf])
    S.op("dve", lambda e: e.tensor_copy(ident, tmpf), reads=[Btmpf], writes=[Bconst])
    S.op("dve", lambda e: e.memset(tmpf, -1.0), reads=[], writes=[Btmpf])
    S.op("pool", lambda e: e.affine_select(out=tmpf, in_=tmpf, pattern=[[-1, 128]], compare_op=ALU.is_ge,
                                           fill=0.0, base=0, channel_multiplier=1), reads=[Btmpf], writes=[Btmpf])
    S.op("dve", lambda e: e.tensor_copy(negU, tmpf), reads=[Btmpf, Bconst], writes=[Bconst])
    S.op("dve", lambda e: e.memset(negOnes, -1.0), reads=[Bconst], writes=[Bconst])
    S.op("dve", lambda e: e.memset(ones_row, 1.0), reads=[Bconst], writes=[Bconst])
    S.op("dve", lambda e: e.memset(zero_row[0:1, :], 0.0), reads=[Bconst], writes=[Bconst])
    S.op("dve", lambda e: e.memset(ssq, 0.0), reads=[Bconst], writes=[Bconst])
    S.op("dve", lambda e: e.memset(lnscr[:, 12:13], -0.5), reads=[Bconst], writes=[Bconst])
    S.op("dve", lambda e: e.memset(s1, 0.0), reads=[Bconst], writes=[Bconst])
    S.op("dve", lambda e: e.memset(s2, 0.0), reads=[Bconst], writes=[Bconst])
    S.op("dve", lambda e: e.memset(fss, 0.0), reads=[Bconst], writes=[Bconst])
    Bmask = Buf("masks")
    S.dma("sp", "cst", masks, masks_in.rearrange("m s t -> s m t"), writes=[Bmask])

    WB = [Buf("w%d" % i) for i in range(3)]

    def wslot(s, pat, **kw):
        return view(OFF_BIG + s * 16384, 16384, BF16, pat, **kw)

    wchunks = []

    def wv_chunk(wmat, c0):
        return [(lambda s: wslot(s, "p (a n) -> p a n", a=16),
                 wmat[:, c0:c0 + 512].rearrange("(a p) n -> p a n", p=128))]

    def wpair_chunk(wmat, c0, c1):
        return [(lambda s: wslot(s, "p (a n) -> p a n", a=16)[:, :, 0:256],
                 wmat[:, c0:c0 + 256].rearrange("(a p) n -> p a n", p=128)),
                (lambda s: wslot(s, "p (a n) -> p a n", a=16)[:, :, 256:512],
                 wmat[:, c1:c1 + 256].rearrange("(a p) n -> p a n", p=128))]

    def wout0_chunk(c0):
        return [(lambda s: wslot(s, "p (a n) -> p a n", a=32),
                 gm_w_out[:, c0:c0 + 256].rearrange("(a p) n -> p a n", p=128))]

    for wc in range(8):
        wchunks.append(wv_chunk(gm_w_in, GW + wc * 512))
    for j in range(16):
        wchunks.append(wpair_chunk(gm_w_in, j * 256, 2 * GW + j * 256))
    for dk in range(8):
        wchunks.append(wout0_chunk(dk * 256))
    if stage >= 2:
        L1_ORDER = [("k", 0), ("q", 0), ("k", 1), ("q", 1), ("k", 2), ("q", 2), ("k", 3), ("q", 3),
                    ("v", 0), ("q", 4), ("v", 1), ("q", 5), ("v", 2), ("q", 6), ("v", 3), ("q", 7)]
        for (kind, idx) in L1_ORDER:
            if kind == "k":
                wchunks.append(wv_chunk(sb_w_in, D + idx * 512))
            elif kind == "v":
                wchunks.append(wv_chunk(sb_w_in, 2 * D + idx * 512))
            else:
                wchunks.append(wpair_chunk(sb_w_in, idx * 256, 3 * D + idx * 256))
        for dk in range(4):
            wchunks.append(wv_chunk(sb_w_out, dk * 512))
    wstate = {"issued": 0, "used": 0}

    def wissue_upto(n):
        while wstate["issued"] <= min(n, len(wchunks) - 1):
            i = wstate["issued"]
            s = i % 3
            for (vf, src) in wchunks[i]:
                S.dma("pool", "w%d" % s, vf(s), src, reads=wdefer.get(i, []), writes=[WB[s]])
            wstate["issued"] += 1

    def wnext():
        i = wstate["used"]
        wissue_upto(i + (0 if i == 0 else 2))
        wstate["used"] += 1
        return i % 3

    wdefer = {}
    wissue_upto(0)

    def norm_setup(src_rows, g_src, dstT, BdstT, ss_col0, src_bufs, pbase, dve_stats=True, split=False):
        R_temp.flip()
        xsl = [view(OFF_TEMP + i * 8192, 8192, F32) for i in range(2)]
        hsl = [view(OFF_TEMP + 16384 + i * 4096, 4096) for i in range(2)]
        gbc = view(OFF_TEMP + 24576, 8192, F32)
        Bx = [R_temp.new("x%d" % i) for i in range(2)]
        Bh = [R_temp.new("h%d" % i) for i in range(2)]
        Bg = R_temp.new("gbc")
        S.dma("sp", "gbc", gbc, g_src.broadcast_to([128, D]), writes=[Bg])

        loaded = set()
        pre_done = set()

        def load(tt):
            pt = PT[tt]
            loaded.add(tt)
            S.dma("sp", "xin%d" % (tt % 2), xsl[tt % 2][:pt, :], src_rows(tt),
                  reads=[src_bufs[tt]] if src_bufs else [], writes=[Bx[tt % 2]])

        def pre(tt):
            pt = PT[tt]
            pre_done.add(tt)
            xv = xsl[tt % 2][:pt, :]
            hv = hsl[tt % 2][:pt, :]
            c = ss_col0 + tt
            if not dve_stats:
                S.op("act", lambda e, hv=hv, xv=xv, c=c, pt=pt: e.activation(hv, xv, AF.Square, accum_out=ssq[:pt, c:c + 1]),
                     reads=[Bx[tt % 2], Bconst], writes=[Bh[tt % 2], Bstat[tt]])
                S.op("act", lambda e, c=c, pt=pt: e.activation(rsd[:pt, c:c + 1], ssq[:pt, c:c + 1], AF.Sqrt, bias=1e-6,
                                                               scale=1.0 / D), reads=[Bstat[tt]], writes=[Bstat[tt]])
                S.op("dve", lambda e, c=c, pt=pt: e.reciprocal(rsd[:pt, c:c + 1], rsd[:pt, c:c + 1]),
                     reads=[Bstat[tt]], writes=[Bstat[tt]])
                return
            S.op("dve", lambda e, hv=hv, xv=xv, c=c, pt=pt: e.scalar_tensor_tensor(
                hv, xv, 1.0, xv, ALU.mult, ALU.mult, accum_out=ssq[:pt, c:c + 1]),
                reads=[Bx[tt % 2], Bconst], writes=[Bh[tt % 2], Bstat[tt]])
            S.op("dve", lambda e, c=c, pt=pt: e.tensor_scalar(rsd[:pt, c:c + 1], ssq[:pt, c:c + 1], 1.0 / D, 1e-6,
                                                              ALU.mult, ALU.add), reads=[Bstat[tt]], writes=[Bstat[tt]])
            S.op("pool", lambda e, c=c, pt=pt: e.tensor_tensor(rsd[:pt, c:c + 1], rsd[:pt, c:c + 1], lnscr[:pt, 12:13],
                                                               ALU.pow), reads=[Bstat[tt], Bconst], writes=[Bstat[tt]])

        def tile(tt):
            pt = PT[tt]
            xv = xsl[tt % 2][:pt, :]
            hv = hsl[tt % 2][:pt, :]
            c = ss_col0 + tt
            if tt not in pre_done:
                pre(tt)
            if dve_stats and tt + 1 < NT and (tt + 1) in loaded and (tt + 1) not in pre_done:
                pre(tt + 1)
            S.op("dve", lambda e, hv=hv, xv=xv, c=c, pt=pt: e.scalar_tensor_tensor(
                hv, xv, rsd[:pt, c:c + 1], gbc[:pt, :], ALU.mult, ALU.mult),
                reads=[Bx[tt % 2], Bstat[tt], Bg], writes=[Bh[tt % 2]])
            if not split:
                back(tt)

        def back(tt):
            pt = PT[tt]
            hv = hsl[tt % 2][:pt, :]
            for half in range(2):
                pb = pbase + half
                pv = bank(pb).bitcast(BF16).rearrange("p (a t) -> p a t", a=8)

                def trs(e, hv=hv, pv=pv, half=half, pt=pt):
                    ins = None
                    for j in range(8):
                        dc = half * 8 + j
                        ins = e.transpose(pv[:, j, :pt], hv[:, dc * 128:(dc + 1) * 128], ident[:pt, :pt])
                    return ins
                S.op("pe", trs, reads=[Bh[tt % 2], Bconst], writes=[PSB[pb]])
                dv = dstT[:, half * 8:(half + 1) * 8, tt * 128:tt * 128 + pt]
                if half == 0:
                    S.op("act", lambda e, dv=dv, pv=pv, pt=pt: e.copy(dv, pv[:, :, :pt]), reads=[PSB[pb]], writes=[BdstT[tt]])
                else:
                    S.op("dve", lambda e, dv=dv, pv=pv, pt=pt: e.tensor_copy(dv, pv[:, :, :pt]), reads=[PSB[pb]],
                         writes=[BdstT[tt]])
        if split:
            return load, tile, back
        return load, tile

    hT = view(OFF_HI, 33792, BF16, "p (a t) -> p a t", a=16)
    BhT = [R_hi.new("hT%d" % i) for i in range(NT)]
    wdefer[1] = [BhT[1]]
    wdefer[2] = [BhT[3]]

    def x_rows(tt):
        return xp[tt * 128:(tt + 1) * 128, :] if tt < 8 else xs[:, :]
    a_load, a_tile = norm_setup(x_rows, norm_g[0:1, :], hT, BhT, 0, None, 0)

    lng = view(OFF_TEMP, 16384, F32)
    OFF_SP = OFF_BIG + 82944
    t1s = [view(OFF_SP + i * 4096, 4096, F32, "p (a k) -> p a k", a=8) for i in range(2)]
    sstg = view(OFF_SP + 8192, 2048, F32, "p (a k) -> p a k", a=4)
    lnb32 = view(OFF_SP + 10240, 2048, F32, "p (a k) -> p a k", a=4)
    junk = view(OFF_SP + 8192, 1024, BF16, "p (a k) -> p a k", a=4)
    wst = view(OFF_SP + 12288, 4096, BF16, "p (g s) -> p g s", g=16)
    Bt1 = [R_hi.new("t1_%d" % i) for i in range(2)]
    Bsstg = R_hi.new("sstg")
    Blnb32 = R_hi.new("lnb32")
    Bjunk = Bsstg
    Bwst = R_hi.new("wst")
    vn16 = view(OFF_VN, VN_SZ, F16, "p (c t k) -> p c t k", c=32, t=9)
    vnb = view(OFF_VN, VN_SZ, BF16, "p (c t k) -> p c t k", c=32, t=9)
    Bvn = [[R_vn.new("vn%d_%d" % (cc, tt)) for tt in range(NT)] for cc in range(32)]
    Bvn_cc = None

    def ln_stats(tt):
        pt = PT[tt]
        st = Bstat[tt]
        scrA = lnscr[:pt, 0:8].rearrange("p (k w) -> p k w", k=2)
        scrB = lnscr[:pt, 8:12].rearrange("p (k w) -> p k w", k=2)
        S.op("dve", lambda e: e.tensor_tensor(scrA, s12[:pt, :, tt, 0:4], s12[:pt, :, tt, 4:8], ALU.add),
             reads=[st], writes=[Blnscr])
        S.op("dve", lambda e: e.tensor_tensor(scrB, scrA[:, :, 0:2], scrA[:, :, 2:4], ALU.add),
             reads=[Blnscr], writes=[Blnscr])
        S.op("dve", lambda e: e.tensor_tensor(mv[:pt, tt, 0:2].rearrange("p (k o) -> p k o", o=1), scrB[:, :, 0:1],
                                               scrB[:, :, 1:2], ALU.add), reads=[Blnscr, st], writes=[st])
        S.op("dve", lambda e: e.tensor_scalar(mv[:pt, tt, 2:3], mv[:pt, tt, 0:1], 1.0 / GW, None, ALU.mult),
             reads=[st], writes=[st])
        S.op("dve", lambda e: e.tensor_tensor(mv[:pt, tt, 3:4], mv[:pt, tt, 2:3], mv[:pt, tt, 2:3], ALU.mult),
             reads=[st], writes=[st])
        S.op("dve", lambda e: e.tensor_scalar(mv[:pt, tt, 6:7], mv[:pt, tt, 1:2], 1.0 / GW, None, ALU.mult),
             reads=[st], writes=[st])
        S.op("dve", lambda e: e.tensor_tensor(mv[:pt, tt, 4:5], mv[:pt, tt, 6:7], mv[:pt, tt, 3:4], ALU.subtract),
             reads=[st], writes=[st])
        S.op("dve", lambda e: e.tensor_scalar(mv[:pt, tt, 7:8], mv[:pt, tt, 4:5], 1e-5, None, ALU.add),
             reads=[st], writes=[st])
        S.op("pool", lambda e: e.tensor_tensor(mv[:pt, tt, 5:6], mv[:pt, tt, 7:8], lnscr[:pt, 12:13], ALU.pow),
             reads=[st, Bconst], writes=[st])
    def ln_norm(tt):
        pt = PT[tt]
        st = Bstat[tt]
        S.op("dve", lambda e: e.scalar_tensor_tensor(mv[:pt, tt, 3:4], mv[:pt, tt, 2:3], -1.0, mv[:pt, tt, 5:6],
                                                     ALU.mult, ALU.mult), reads=[st], writes=[st])
        if tt < 8:
            for pc in range(4):
                t1 = t1s[pc % 2]
                gsl = lng[:pt, pc * 1024:(pc + 1) * 1024].rearrange("p (a k) -> p a k", a=8)
                src = vn16[:pt, pc * 8:(pc + 1) * 8, tt, :]
                dst = vnb[:pt, pc * 8:(pc + 1) * 8, tt, :]
                vb = [Bvn[cc][tt] for cc in range(pc * 8, pc * 8 + 8)]
                S.op("act", lambda e, t1=t1, src=src: e.activation(t1[:pt], src, AF.Identity, scale=mv[:pt, tt, 5:6],
                                                                   bias=mv[:pt, tt, 3:4]),
                     reads=vb + [st], writes=[Bt1[pc % 2]])
                S.op("dve", lambda e, t1=t1, dst=dst, gsl=gsl: e.tensor_tensor(dst, t1[:pt], gsl, ALU.mult),
                     reads=[Bt1[pc % 2], Blng], writes=vb)
        else:
            for pc in range(8):
                t1 = t1s[pc % 2][:, 0:4, :]
                gsl = lng[:pt, pc * 512:(pc + 1) * 512].rearrange("p (a k) -> p a k", a=4)
                src = vn16[:pt, pc * 4:(pc + 1) * 4, tt, :]
                dst = vnb[:pt, pc * 4:(pc + 1) * 4, tt, :]
                vb = [Bvn[cc][tt] for cc in range(pc * 4, pc * 4 + 4)]
                S.dma("sp", "lnb32", lnb32[:pt], gm_ln_b[:, pc * 512:(pc + 1) * 512].broadcast_to([32, 512])
                      .rearrange("p (a k) -> p a k", a=4), writes=[Blnb32])
                S.op("act", lambda e, t1=t1, src=src: e.activation(t1[:pt], src, AF.Identity, scale=mv[:pt, tt, 5:6],
                                                                   bias=mv[:pt, tt, 3:4]),
                     reads=vb + [st], writes=[Bt1[pc % 2]])
                S.op("dve", lambda e, t1=t1, gsl=gsl: e.tensor_tensor(sstg[:pt], t1[:pt], gsl, ALU.mult),
                     reads=[Bt1[pc % 2], Blng], writes=[Bsstg])
                S.op("act", lambda e, dst=dst: e.copy(dst, sstg[:pt]), reads=[Bsstg], writes=vb)
                S.op("dve", lambda e: e.tensor_tensor(sstg[:pt], sstg[:pt], lnb32[:pt], ALU.add),
                     reads=[Bsstg, Blnb32], writes=[Bsstg])
                S.dma("sp", "gmv", gmv_o[:, pc * 512:(pc + 1) * 512].rearrange("p (a k) -> p a k", a=4),
                      sstg[:pt], reads=[Bsstg], final=True)

    pbk = 0
    Blng = None
    hist_after_A = None
    a_load(0)
    for wc in range(8):
        s = wnext()
        wv = wslot(s, "p (a n) -> p a n", a=16)
        if wc == 6:
            S.dma("pool", "wst", wst, gm_w_s.rearrange("(g t) s -> t g s", g=16), writes=[Bwst])
            S.op("dve", lambda e: e.memset(wst[0:64, :, 64:128], 0.0), reads=[Bwst], writes=[Bwst])
        if wc == 1:
            R_temp.flip()
            hist_after_A = dict(R_temp.hist)
            Blng = R_temp.new("lng")
            S.dma("sp", "lngb", lng, gm_ln_g.broadcast_to([128, GW]), writes=[Blng])
        for tt in range(NT):
            pt = PT[tt]
            if wc == 0:
                if tt + 1 < NT:
                    a_load(tt + 1)
                a_tile(tt)
                if tt == 1:
                    wissue_upto(1)
                if tt == 3:
                    wissue_upto(2)
            pb = 2 + (pbk % 4)
            pbk += 1
            po = bank(pb)

            def mm(e, wv=wv, po=po, tt=tt, pt=pt):
                ins = None
                for dc in range(16):
                    ins = e.matmul(po[:pt, :], hT[:, dc, tt * 128:tt * 128 + pt], wv[:, dc, :],
                                   start=(dc == 0), stop=(dc == 15))
                return ins
            S.op("pe", mm, reads=[BhT[tt], WB[s]], writes=[PSB[pb]])
            vb = [Bvn[cc][tt] for cc in range(wc * 4, wc * 4 + 4)]
            gdst = vn16[:pt, wc * 4:(wc + 1) * 4, tt, :]
            S.op("act", lambda e, gdst=gdst, po=po, pt=pt, tt=tt, wc=wc: e.activation(
                gdst, po[:pt, :].rearrange("p (a k) -> p a k", a=4), AF.Gelu_apprx_tanh,
                accum_out=s1[:pt, tt, wc:wc + 1]), reads=[PSB[pb], Bconst], writes=vb + [Bstat[tt]])
            S.op("dve", lambda e, gdst=gdst, pt=pt, tt=tt, wc=wc: e.scalar_tensor_tensor(
                junk[:pt], gdst, 1.0, gdst, ALU.mult, ALU.mult, accum_out=s2[:pt, tt, wc:wc + 1]),
                reads=vb, writes=[Bjunk, Bstat[tt]])
            if wc == 7:
                ln_stats(tt)
                if tt >= 1:
                    ln_norm(tt - 1)
                if tt == NT - 1:
                    ln_norm(tt)

    gus = [view(OFF_TEMP + 16384 + i * 4224, 2112) for i in range(3)]
    szs = [view(OFF_TEMP + 16384 + i * 4224 + 2112, 2112) for i in range(3)]
    Bgu = [Buf("gu%d" % i, hist_after_A) for i in range(3)]
    Bsz = [Buf("sz%d" % i, hist_after_A) for i in range(3)]
    R_temp.flip()
    R_temp.live.extend(Bgu + Bsz)
    t1c = [view(OFF_TEMP, 4224, F32)] * 2
    rhs2 = view(OFF_TEMP + 4224, 8192, F32)
    rhs2s = view(OFF_TEMP + 12416, 2048, F32)
    lhs2 = [view(OFF_TEMP + 14464 + i * 512, 512, F32) for i in range(2)]
    Bt1c = [R_temp.new("t1c")] * 2
    Brhs2 = R_temp.new("rhs2")
    Brhs2s = R_temp.new("rhs2s")
    Blhs2 = [R_temp.new("lhs2_%d" % i) for i in range(2)]
    sp_init = {}
    for b in (Bt1[0], Bt1[1], Bsstg, Blnb32, Bjunk):
        for k, v in b.alldeps().items():
            if sp_init.get(k, 0) < v:
                sp_init[k] = v
    gus.append(view(OFF_SP, 2112))
    szs.append(view(OFF_SP + 2112, 2112))
    Bgu.append(Buf("gu3", sp_init))
    Bsz.append(Buf("sz3", sp_init))
    wmT = view(OFF_SP + 4224, 4096, BF16, "p (g t) -> p g t", g=16)
    BwmT = Buf("wmT", sp_init)
    S.dma("sp", "bsr", rhs2[1:2, :], gm_b_s, writes=[Brhs2])
    S.dma("sp", "bsr", rhs2s[1:2, 0:512].rearrange("p (g t) -> p g t", g=16),
          gm_b_s.rearrange("p (g t) -> p g t", g=16)[:, :, 0:32], reads=[Brhs2], writes=[Brhs2s])
    for i in range(2):
        S.op("dve", lambda e, i=i: e.memset(lhs2[i][0:2, :], 1.0), writes=[Blhs2[i]])

    def lhs2_load(cc):
        S.dma("sp", "lhs2_%d" % (cc % 2), lhs2[cc % 2][0:1, :], gm_ln_b[:, cc * 128:(cc + 1) * 128], writes=[Blhs2[cc % 2]])

    def emit_wmT():
        for half in range(2):
            pv = bank(6 + half).bitcast(BF16).rearrange("p (a t) -> p a t", a=8)

            def trw(e, pv=pv, half=half):
                ins = None
                for j in range(8):
                    ins = e.transpose(pv[:, j, :], wst[:, half * 8 + j, :], ident)
                return ins
            S.op("pe", trw, reads=[Bwst, Bconst], writes=[PSB[6 + half]])
            S.op("dve", lambda e, pv=pv, half=half: e.tensor_copy(wmT[:, half * 8:(half + 1) * 8, :], pv),
                 reads=[PSB[6 + half]], writes=[BwmT])
            yield
        for q4 in range(4):
            pwq = bank(6 + q4 % 2)
            S.op("pe", lambda e, q4=q4, pwq=pwq: e.matmul(pwq[0:1, :].rearrange("p (g t) -> p g t", g=4), negOnes[:, 0:1],
                                                          wmT[:, q4 * 4:(q4 + 1) * 4, :], start=True, stop=True),
                 reads=[BwmT, Bconst], writes=[PSB[6 + q4 % 2]])
            S.op("act", lambda e, q4=q4, pwq=pwq: e.mul(rhs2[0:1, q4 * 512:(q4 + 1) * 512], pwq[0:1, :], -1.0),
                 reads=[PSB[6 + q4 % 2]], writes=[Brhs2])
            yield
        pw = bank(6)
        S.op("pe", lambda e: e.matmul(pw[0:1, :].rearrange("p (g t) -> p g t", g=16), negOnes[0:32, 0:1],
                                      wmT[0:32, :, 0:32], start=True, stop=True),
             reads=[BwmT, Bconst], writes=[PSB[6]])
        S.op("act", lambda e: e.mul(rhs2s[0:1, 0:512], pw[0:1, :], -1.0),
             reads=[PSB[6]], writes=[Brhs2s])
        yield

    def yT(cc):
        return view(OFF_VN + cc * 2304, 2112)

    psU = bank(0, 3)
    psZ = bank(3, 3)
    psM = bank(6, 2)
    psMs = bank(0)[:, 384:416]
    psU3 = psU.rearrange("p (r n) -> p r n", r=3)[:, :, 0:352]
    psZ3 = psZ.rearrange("p (r n) -> p r n", r=3)[:, :, 0:352]
    NR = [(0, 352), (352, 352), (704, 352)]
    gus4, szs4, Bgu4, Bsz4 = gus, szs, Bgu, Bsz
    NSET = 4
    MIXLAG = 2
    cstate = {"s": None}
    setup_gen = emit_wmT()

    def emit_uz(cc):
        q = cc % 2
        if q == 0:
            cstate["s"] = wnext()
        s_ = cstate["s"]
        wv = wslot(s_, "p (a n) -> p a n", a=16)
        k4 = cc % NSET
        for (which, col0, pst, bst) in ((0, q * 128, psU, [PSB[0], PSB[1], PSB[2]]),
                                        (1, 256 + q * 128, psZ, [PSB[3], PSB[4], PSB[5]])):
            def mmuz(e, wv=wv, col0=col0, pst=pst):
                ins = None
                for r, (n0, n) in enumerate(NR):
                    o = pst[:, r * 512:r * 512 + n]
                    for dc in range(16):
                        ins = e.matmul(o, wv[:, dc, col0:col0 + 128], hT[:, dc, n0:n0 + n],
                                       start=(dc == 0), stop=(dc == 15))
                return ins
            S.op("pe", mmuz, reads=BhT + [WB[s_]], writes=bst)
            next(setup_gen, None)
        gu = gus4[k4]
        sz = szs4[k4]
        S.op("act", lambda e, gu=gu: e.activation(gu.rearrange("p (r n) -> p r n", r=3), psU3, AF.Gelu_apprx_tanh),
             reads=[PSB[0], PSB[1], PSB[2]], writes=[Bgu4[k4]])
        S.op("act", lambda e, sz=sz: e.activation(sz.rearrange("p (r n) -> p r n", r=3), psZ3, AF.Silu),
             reads=[PSB[3], PSB[4], PSB[5]], writes=[Bsz4[k4]])

    def emit_mix(cc):
        g = cc // 2
        k4 = cc % NSET
        k2 = cc % 2

        if cc + 1 < 32:
            lhs2_load(cc + 1)
        l2 = lhs2[cc % 2]

        def mmix(e, cc=cc, g=g, l2=l2):
            ins = None
            for tb in range(8):
                o = psM[:, tb * 128:(tb + 1) * 128]
                e.matmul(o, vnb[:, cc, tb, :], wmT[:, g, :], start=True, stop=False)
                ins = e.matmul(o, l2[0:2, :], rhs2[0:2, g * 128:(g + 1) * 128], start=False, stop=True)
            e.matmul(psMs, vnb[0:32, cc, 8, :], wmT[0:32, g, 0:32], start=True, stop=False)
            ins = e.matmul(psMs, l2[0:2, :], rhs2s[0:2, g * 32:(g + 1) * 32], start=False, stop=True)
            return ins
        S.op("pe", mmix, reads=Bvn[cc] + [BwmT, Brhs2, Brhs2s, Blhs2[cc % 2], Bconst], writes=[PSB[6], PSB[7], PSB[0]])
        gu = gus4[k4]
        sz = szs4[k4]
        tc = t1c[k2]
        S.op("dve", lambda e, tc=tc, gu=gu: e.tensor_tensor(tc[:, 1024:1056], psMs, gu[:, 1024:1056], ALU.mult),
             reads=[PSB[0], Bgu4[k4]], writes=[Bt1c[k2]])
        S.op("dve", lambda e, tc=tc, gu=gu: e.tensor_tensor(tc[:, 0:1024], psM, gu[:, 0:1024], ALU.mult),
             reads=[PSB[6], PSB[7], Bgu4[k4], Bt1c[k2]], writes=[Bt1c[k2]])
        yv = yT(cc)
        S.op("dve", lambda e, tc=tc, sz=sz, yv=yv: e.tensor_tensor(yv, tc, sz, ALU.mult),
             reads=[Bt1c[k2], Bsz4[k4]], writes=Bvn[cc])

    for cc in range(32):
        emit_uz(cc)
        if cc == MIXLAG:
            for _ in setup_gen:
                pass
            lhs2_load(0)
        if cc >= MIXLAG:
            emit_mix(cc - MIXLAG)
    for cc in range(32 - MIXLAG, 32):
        emit_mix(cc)

    R_hi.flip()
    for o in [Bwst, BwmT] + Bgu4[3:] + Bsz4[3:]:
        for k, v in o.alldeps().items():
            if R_hi.hist.get(k, 0) < v:
                R_hi.hist[k] = v
    xcs = [view(OFF_HI + i * 1024, 1024, F32) for i in range(4)]
    x1s = [view(OFF_HI + 4096 + i * 1024, 1024, F32) for i in range(4)]
    Bxc = [R_hi.new("xc%d" % i) for i in range(4)]
    Bx1s = [R_hi.new("x1s%d" % i) for i in range(4)]
    Bx1scr = [Buf("x1scr%d" % i) for i in range(NT)]
    items = [(dk, tt) for dk in range(8) for tt in range(NT)]

    def xc_load(i):
        dk, tt = items[i]
        pt = PT[tt]
        S.dma("sp", "xc%d" % (i % 4), xcs[i % 4][:pt, :], x_rows(tt)[:, dk * 256:(dk + 1) * 256], writes=[Bxc[i % 4]])
    xc_load(0)
    xc_load(1)
    cur_s = None
    h1T = view(OFF_BIG + 65536, 33792, BF16, "p (a t) -> p a t", a=16)
    Bh1T = [R_hi.new("h1T%d" % i) for i in range(NT)]

    def x1_rows(tt):
        return x1_scr[tt * 128:tt * 128 + PT[tt], :]
    b_load = b_tile = b_back = None
    for i, (dk, tt) in enumerate(items):
        pt = PT[tt]
        if tt == 0:
            cur_s = wnext()
        if dk == 7 and tt == 0 and stage != 1:
            b_load, b_tile, b_back = norm_setup(x1_rows, norm_g[1:2, :], h1T, Bh1T, 9, Bx1scr, 4, dve_stats=False, split=True)
        wv = wslot(cur_s, "p (a n) -> p a n", a=32)
        if i + 2 < len(items):
            xc_load(i + 2)
        pb = i % 4
        po = bank(pb)

        def mmo(e, wv=wv, po=po, tt=tt, pt=pt):
            ins = None
            for cc in range(32):
                ins = e.matmul(po[:pt, 0:256], yT(cc)[:, tt * 128:tt * 128 + pt], wv[:, cc, :],
                               start=(cc == 0), stop=(cc == 31))
            return ins
        S.op("pe", mmo, reads=[Bvn[cc][0] for cc in range(32)] + [WB[cur_s]], writes=[PSB[pb]])
        S.op("dve", lambda e, po=po, i=i, pt=pt: e.tensor_tensor(x1s[i % 4][:pt, :], po[:pt, 0:256], xcs[i % 4][:pt, :], ALU.add),
             reads=[PSB[pb], Bxc[i % 4]], writes=[Bx1s[i % 4]])
        S.dma("sp", "x1s%d" % (i % 4), x1_scr[tt * 128:tt * 128 + pt, dk * 256:(dk + 1) * 256], x1s[i % 4][:pt, :],
              reads=[Bx1s[i % 4]], writes=[Bx1scr[tt]])
        if b_load is not None:
            b_load(tt)
            if tt >= 3:
                b_back(tt - 3)
            if tt >= 1:
                b_tile(tt - 1)
    l1a_tail = []
    if b_tile is not None:
        b_back(NT - 3)
        b_tile(NT - 1)
        l1a_tail = [lambda: b_back(NT - 2), lambda: b_back(NT - 1)]

    if stage == 1:
        R_temp.flip()
        dbg = view(OFF_TEMP, 8192, F32)
        Bd = R_temp.new("dbg")
        for tt in range(NT):
            pt = PT[tt]
            S.dma("sp", "dbg_l", dbg[:pt, :], x1_scr[tt * 128:tt * 128 + pt, :], reads=[Bx1scr[tt]], writes=[Bd])
            dst = yp_o[tt * 128:(tt + 1) * 128, :] if tt < 8 else ys_o[:, :]
            S.dma("sp", "dbg_s", dst, dbg[:pt, :], reads=[Bd], final=True)
        S.build()
        return nc

    if stage == 6:
        S.build()
        return nc
    R_temp.flip()
    stgf = [view(OFF_HI + i * 2048, 2048, F32) for i in range(3)]
    kb16 = [view(OFF_HI + 6144 + i * 1024, 1024) for i in range(2)]
    vstg = [view(OFF_HI + 8192 + i * 1024, 1024) for i in range(3)]
    R_hi_low = Region("hi_low")
    for b in Bxc + Bx1s:
        for k, v in b.alldeps().items():
            if R_hi_low.hist.get(k, 0) < v:
                R_hi_low.hist[k] = v
    Bstgf = [R_hi_low.new("stgf%d" % i) for i in range(3)]
    Bkb16 = [R_hi_low.new("kb16_%d" % i) for i in range(2)]
    Bvstg = [R_hi_low.new("vstg%d" % i) for i in range(3)]
    ktst = [view(OFF_TEMP + i * 8192, 8192, BF16, "p (h s) -> p h s", h=4) for i in range(2)]
    Bktst = [R_temp.new("ktst%d" % i) for i in range(2)]
    vs_new = view(OFF_TEMP + 16384, 4096)
    kts_new = view(OFF_TEMP + 20480, 1024, BF16, "p (h s) -> p h s", h=16)
    Bvsn = R_temp.new("vs_new")
    Bktsn = R_temp.new("kts_new")
    Bktsend = [Buf("kt_send%d" % i) for i in range(4)]
    Bvsend = [Buf("v_send%d" % i) for i in range(4)]
    Bktall = [Buf("kt_all%d" % i) for i in range(4)]
    Bvall = [Buf("v_all%d" % i) for i in range(4)]
    RG = [[0, 1], [2, 3], [4, 5], [6, 7]]
    Bccser = Buf("cc_serial")
    bstate = {"cnt": 0}
    pending_tr = []

    def emit_kv(kv, hc):
        cnt = bstate["cnt"]
        if True:
            s = wnext()
            wv = wslot(s, "p (a n) -> p a n", a=16)
            for tt in range(NT):
                pt = PT[tt]
                if l1a_tail and tt >= 2:
                    l1a_tail.pop(0)()
                pb = 2 + (cnt % 4)
                po = bank(pb)

                def mm(e, wv=wv, po=po, tt=tt, pt=pt):
                    ins = None
                    for dc in range(16):
                        ins = e.matmul(po[:pt, :], h1T[:, dc, tt * 128:tt * 128 + pt], wv[:, dc, :],
                                       start=(dc == 0), stop=(dc == 15))
                    return ins
                S.op("pe", mm, reads=[Bh1T[tt], WB[s]], writes=[PSB[pb]])
                sf = stgf[cnt % 3]
                S.op("act", lambda e, sf=sf, po=po, pt=pt: e.copy(sf[:pt, :], po[:pt, :]), reads=[PSB[pb]],
                     writes=[Bstgf[cnt % 3]])
                if tt < 8:
                    odst = (kp_o if kv == 0 else vp_o)[tt * 128:(tt + 1) * 128, hc * 512:(hc + 1) * 512]
                else:
                    odst = (ks_o if kv == 0 else vs_o)[:, hc * 512:(hc + 1) * 512]
                S.dma("sp", "stgf%d" % (cnt % 3), odst, sf[:pt, :], reads=[Bstgf[cnt % 3]], final=True)
                if kv == 1:
                    if tt < 8:
                        vg = vstg[cnt % 3]
                        S.op("dve", lambda e, vg=vg, sf=sf, pt=pt: e.tensor_copy(vg[:pt, :], sf[:pt, :]), reads=[Bstgf[cnt % 3]],
                             writes=[Bvstg[cnt % 3]])
                        S.dma("sp", "vstg%d" % (cnt % 3), v_send[hc].ap()[tt * 128:(tt + 1) * 128, :],
                              vg[:pt, :], reads=[Bvstg[cnt % 3]], writes=[Bvsend[hc]])
                    else:
                        S.op("dve", lambda e, sf=sf, pt=pt, hc=hc: e.tensor_copy(vs_new[:pt, hc * 512:(hc + 1) * 512], sf[:pt, :]),
                             reads=[Bstgf[cnt % 3]], writes=[Bvsn])
                else:
                    kb = kb16[cnt % 2]
                    S.op("dve", lambda e, kb=kb, sf=sf, pt=pt: e.tensor_copy(kb[:pt, :], sf[:pt, :]), reads=[Bstgf[cnt % 3]],
                         writes=[Bkb16[cnt % 2]])
                    def do_tr(kb=kb, pt=pt, tt=tt, hc=hc, cnt=cnt):
                        pt_b = cnt % 2
                        pv = bank(pt_b).bitcast(BF16).rearrange("p (a t) -> p a t", a=8)

                        def trk(e, kb=kb, pv=pv, pt=pt):
                            ins = None
                            for hq in range(4):
                                ins = e.transpose(pv[:, hq, :pt], kb[:pt, hq * 128:(hq + 1) * 128], ident[:pt, :pt])
                            return ins
                        S.op("pe", trk, reads=[Bkb16[cnt % 2], Bconst], writes=[PSB[pt_b]])
                        if tt < 8:
                            kd = ktst[hc % 2][:, :, tt * 128:(tt + 1) * 128]
                            S.op("act", lambda e, kd=kd, pv=pv: e.copy(kd, pv[:, 0:4, :]), reads=[PSB[pt_b]],
                                 writes=[Bktst[hc % 2]])
                        else:
                            kd = kts_new[:, hc * 4:(hc + 1) * 4, :]
                            S.op("act", lambda e, kd=kd, pv=pv: e.copy(kd, pv[:, 0:4, 0:32]), reads=[PSB[pt_b]],
                                 writes=[Bktsn])
                    if pending_tr:
                        pending_tr.pop()()
                    pending_tr.append(do_tr)
                cnt += 1
            if pending_tr:
                pending_tr.pop()()
            if kv == 0:
                S.dma("sp", "ktst%d" % (hc % 2), kt_send[hc].ap().rearrange("(h p) s -> p h s", p=128), ktst[hc % 2],
                      reads=[Bktst[hc % 2]], writes=[Bktsend[hc]])
                if stage != 7:
                    S.custom("pool", "cck%d" % hc, 1, lambda e, hc=hc: e.collective_compute(
                        "AllGather", ALU.bypass, replica_groups=RG,
                        ins=[kt_send[hc].ap().opt()], outs=[kt_all[hc].ap().opt()]), reads=[Bktsend[hc]], writes=[Bktall[hc], Bccser])
            else:
                if stage != 7:
                    S.custom("pool", "ccv%d" % hc, 1, lambda e, hc=hc: e.collective_compute(
                        "AllGather", ALU.bypass, replica_groups=RG,
                        ins=[v_send[hc].ap().opt()], outs=[v_all[hc].ap().opt()]), reads=[Bvsend[hc]], writes=[Bvall[hc], Bccser])
            bstate["cnt"] = cnt

    R_vn.flip()
    QT = view(OFF_VN, 33792, BF16, "p (h t) -> p h t", h=16)
    sZT = view(OFF_VN + 33792, 33792, BF16, "p (h t) -> p h t", h=16)
    BQT = [R_vn.new("QT%d" % h) for h in range(16)]
    BsZ = [R_vn.new("sZ%d" % h) for h in range(16)]
    QT3 = QT.rearrange("p h (r n) -> p h r n", r=3)
    sZT3 = sZT.rearrange("p h (r n) -> p h r n", r=3)
    def emit_qz(j):
        s = wnext()
        wv = wslot(s, "p (a n) -> p a n", a=16)
        for q in range(2):
            h = 2 * j + q
            for (which, col0, pst, bst) in ((0, q * 128, psU, [PSB[0], PSB[1], PSB[2]]),
                                            (1, 256 + q * 128, psZ, [PSB[3], PSB[4], PSB[5]])):
                def mmqz(e, wv=wv, col0=col0, pst=pst):
                    ins = None
                    for r, (n0, n) in enumerate(NR):
                        o = pst[:, r * 512:r * 512 + n]
                        for dc in range(16):
                            ins = e.matmul(o, wv[:, dc, col0:col0 + 128], h1T[:, dc, n0:n0 + n],
                                           start=(dc == 0), stop=(dc == 15))
                    return ins
                S.op("pe", mmqz, reads=Bh1T + [WB[s]], writes=bst)
            S.op("dve", lambda e, h=h: e.tensor_copy(QT3[:, h, :, :], psU3), reads=[PSB[0], PSB[1], PSB[2]], writes=[BQT[h]])
            S.op("act", lambda e, h=h: e.activation(sZT3[:, h, :, :], psZ3, AF.Silu), reads=[PSB[3], PSB[4], PSB[5]],
                 writes=[BsZ[h]])

    for (kind, idx) in L1_ORDER:
        if kind == "k":
            emit_kv(0, idx)
        elif kind == "v":
            emit_kv(1, idx)
        else:
            emit_qz(idx)

    R_hi.flip()
    for rg in (R_hi_low,):
        rg.flip()
        for k, v in rg.hist.items():
            if R_hi.hist.get(k, 0) < v:
                R_hi.hist[k] = v
    R_temp.flip()
    A0 = OFF_HI
    KTp = [view(A0 + i * 8192, 8192, BF16, "p (q r s) -> p q r s", q=2, r=2) for i in range(2)]
    Vp = [view(A0 + 16384 + i * 8192, 8192, BF16, "p (r k) -> p r k", r=16) for i in range(2)]
    BKTp = [R_hi.new("KTp%d" % i) for i in range(2)]
    BVp = [R_hi.new("Vp%d" % i) for i in range(2)]
    set_off = {(0, 0): A0 + 32768, (0, 1): A0 + 38912, (1, 0): OFF_TEMP, (1, 1): OFF_TEMP + 6144}
    e_t, ec_t, sp_t, w_t, Be, Bec, Bsp, Bw = {}, {}, {}, {}, {}, {}, {}, {}
    for key, off in set_off.items():
        reg = R_hi if key[0] == 0 else R_temp
        e_t[key] = view(off, 2048, F32)
        ec_t[key] = view(off + 2048, 2048, F32)
        sp_t[key] = view(off + 4096, 1024)
        w_t[key] = view(off + 5120, 1024)
        Be[key], Bec[key], Bsp[key], Bw[key] = (reg.new("e%s" % (key,)), reg.new("ec%s" % (key,)),
                                                reg.new("sp%s" % (key,)), reg.new("w%s" % (key,)))
    fac = {(s, j): view(A0 + 45056 + (2 * s + j) * 1024, 1024) for s in range(2) for j in range(2)}
    Bfac = {key: R_hi.new("fac%s" % (key,)) for key in fac}
    se_t = view(OFF_TEMP + 12288, 1152, F32)
    sec_t = view(OFF_TEMP + 13440, 1152, F32)
    ssp_t = view(OFF_TEMP + 14592, 576)
    sw_t = view(OFF_TEMP + 15168, 576)
    ckst = view(OFF_TEMP + 21504, 2048, BF16, "p (b k) -> p b k", b=8)
    cKT = [view(OFF_TEMP + 23552 + i * 2048, 2048) for i in range(2)]
    cVh = [view(OFF_TEMP + 27648 + i * 2048, 2048, BF16, "p (b k) -> p b k", b=8) for i in range(2)]
    Bckst = R_temp.new("ckst")
    BcKT = [R_temp.new("cKT%d" % i) for i in range(2)]
    BcVh = [R_temp.new("cVh%d" % i) for i in range(2)]
    Bse, Bsec, Bssp, Bsw = R_temp.new("se"), R_temp.new("sec"), R_temp.new("ssp"), R_temp.new("sw")

    def kv_load(pi):
        i = pi % 2
        hc, hq0 = pi // 2, (2 * pi) % 4
        for q in range(2):
            S.dma("sp", "ktp%d" % i, KTp[i][:, q, :, :],
                  kt_all[hc].ap().rearrange("(r q d) s -> d q r s", r=2, q=4)[:, hq0 + q, :, :],
                  reads=[Bktall[hc]], writes=[BKTp[i]])
        S.dma("sp", "vp%d" % i, Vp[i],
              v_all[hc].ap().rearrange("(r s) c -> s r c", s=128)[:, :, hq0 * 128:hq0 * 128 + 256],
              reads=[Bvall[hc]], writes=[BVp[i]])

    def ck_load(h):
        S.dma("pool", "ckst", ckst, ck[:, h * 128:(h + 1) * 128].rearrange("(b s) k -> s b k", s=128), writes=[Bckst])

    def cv_load(h):
        i = h % 2
        S.dma("pool", "cvh%d" % i, cVh[i], cv[:, h * 128:(h + 1) * 128].rearrange("(b s) k -> s b k", s=128),
              writes=[BcVh[i]])

    TILES = []
    for gq in range(2):
        kb_max = 8 * gq + 7
        for kb in range(kb_max, -1, -1):
            i_min = max(4 * gq, kb // 2)
            c0, c1 = i_min * 128, (4 * gq + 4) * 128
            ib, rb = kb // 2, kb % 2
            TILES.append(dict(gq=gq, kb=kb, c0=c0, c1=c1, N=c1 - c0, r0=c0 - gq * 512, masked=(kb // 2) >= 4 * gq,
                              ib=ib, rb=rb, p_own=(ib + rb) % 2, first=(kb == kb_max), last=(kb == 0)))
    NTL = len(TILES)

    def sample_head(h):
        i = h % 2
        pv = bank(7).bitcast(BF16)

        def trc(e):
            ins = None
            for b in range(8):
                ins = e.transpose(pv[:, b * 128:(b + 1) * 128], ckst[:, b, :], ident)
            return ins
        S.op("pe", trc, reads=[Bckst, Bconst], writes=[PSB[7]])
        yield
        S.op("act", lambda e: e.copy(cKT[i], pv), reads=[PSB[7]], writes=[BcKT[i]])
        if h + 1 < 16:
            ck_load(h + 1)
        yield
        pL = bank(6)
        qs = QT[:, h, 1024:1056]

        def mml(e):
            ins = None
            for b in range(8):
                ins = e.matmul(pL[:, b * 32:(b + 1) * 32], cKT[i][:, b * 128:(b + 1) * 128], qs, start=True, stop=True)
            ins = e.matmul(pL[0:32, 256:288], kts_new[:, h, :], qs, start=True, stop=True)
            return ins
        S.op("pe", mml, reads=[BcKT[i], Bktsn, BQT[h]], writes=[PSB[6]])
        yield
        S.op("act", lambda e: e.activation(se_t[:, 0:256], pL[:, 0:256], AF.Exp, scale=SCALE), reads=[PSB[6]], writes=[Bse])
        S.op("act", lambda e: e.activation(se_t[0:32, 256:288], pL[0:32, 256:288], AF.Exp, scale=SCALE),
             reads=[PSB[6], Bse], writes=[Bse])
        yield
        S.op("dve", lambda e: e.tensor_tensor(se_t[0:32, 256:288], se_t[0:32, 256:288], masks[0:32, 4, 0:32], ALU.mult),
             reads=[Bse, Bmask], writes=[Bse])
        yield
        S.op("act", lambda e: e.activation(ssp_t[:, 0:256], se_t[:, 0:256], AF.Ln, bias=1.0), reads=[Bse], writes=[Bssp])
        S.op("act", lambda e: e.activation(ssp_t[0:32, 256:288], se_t[0:32, 256:288], AF.Ln, bias=1.0),
             reads=[Bse, Bssp], writes=[Bssp])
        yield

        def mmc(e):
            pc = bank(7)
            ins = None
            for b in range(8):
                o = pc[:, b * 32:(b + 1) * 32]
                ins = e.matmul(o, negU, ssp_t[:, b * 32:(b + 1) * 32], start=True, stop=False)
                for b2 in range(b + 1, 8):
                    ins = e.matmul(o, negOnes, ssp_t[:, b2 * 32:(b2 + 1) * 32], start=False, stop=False)
                ins = e.matmul(o, negOnes[0:32, :], ssp_t[0:32, 256:288], start=False, stop=True)
            ins = e.matmul(pc[0:32, 256:288], negU[0:32, 0:32], ssp_t[0:32, 256:288], start=True, stop=True)
            return ins
        S.op("pe", mmc, reads=[Bssp, Bconst, BcKT[i]], writes=[PSB[7]])
        yield
        pc = bank(7)
        S.op("act", lambda e: e.activation(sec_t[:, 0:256], pc[:, 0:256], AF.Exp), reads=[PSB[7]], writes=[Bsec])
        S.op("act", lambda e: e.activation(sec_t[0:32, 256:288], pc[0:32, 256:288], AF.Exp), reads=[PSB[7], Bsec], writes=[Bsec])
        yield
        S.op("dve", lambda e: e.tensor_tensor(sw_t[:, 0:256], se_t[:, 0:256], sec_t[:, 0:256], ALU.mult),
             reads=[Bse, Bsec], writes=[Bsw])
        S.op("dve", lambda e: e.tensor_tensor(sw_t[0:32, 256:288], se_t[0:32, 256:288], sec_t[0:32, 256:288], ALU.mult),
             reads=[Bse, Bsec, Bsw], writes=[Bsw])
        yield
        pO = bank(6)

        def mmo(e):
            ins = None
            for b in range(8):
                ins = e.matmul(pO[:, 320:352], cVh[i][:, b, :], sw_t[:, b * 32:(b + 1) * 32], start=(b == 0), stop=False)
            ins = e.matmul(pO[:, 320:352], vs_new[0:32, h * 128:(h + 1) * 128], sw_t[0:32, 256:288], start=False, stop=True)
            return ins
        S.op("pe", mmo, reads=[BcVh[i], Bvsn, Bsw], writes=[PSB[6]])
        if h + 2 < 16:
            cv_load(h + 2)
        yield
        S.op("dve", lambda e: e.tensor_tensor(sZT[:, h, 1024:1056], pO[:, 320:352], sZT[:, h, 1024:1056], ALU.mult),
             reads=[PSB[6], BsZ[h]], writes=[BsZ[h]])
        yield

    def emit_pair(pi):
        kvb = pi % 2
        heads = (2 * pi, 2 * pi + 1)
        fcur = {0: 0, 1: 0}

        def gen_samples():
            for h in heads:
                for _ in sample_head(h):
                    yield
        sg = gen_samples()

        def st_L(s, t):
            T_ = TILES[t]
            pL = bank(s)
            S.op("pe", lambda e, pL=pL, T_=T_, s=s: e.matmul(
                pL[:, 0:T_["N"]], KTp[kvb][:, s, T_["p_own"], T_["ib"] * 128:(T_["ib"] + 1) * 128],
                QT[:, heads[s], T_["c0"]:T_["c1"]], start=True, stop=True),
                reads=[BKTp[kvb], BQT[heads[s]]], writes=[PSB[s]])

        def st_ExpL(s, t):
            T_ = TILES[t]
            key = (s, t % 2)
            S.op("act", lambda e, key=key, T_=T_, s=s: e.activation(e_t[key][:, 0:T_["N"]], bank(s)[:, 0:T_["N"]], AF.Exp,
                                                                   scale=SCALE), reads=[PSB[s]], writes=[Be[key]])

        def st_mask(s, t):
            T_ = TILES[t]
            if not T_["masked"]:
                return
            key = (s, t % 2)
            mi = (T_["ib"] % 2) * 2 + T_["rb"]
            S.op("dve", lambda e, key=key, mi=mi: e.tensor_tensor(e_t[key][:, 0:128], e_t[key][:, 0:128], masks[:, mi, :],
                                                                 ALU.mult), reads=[Be[key], Bmask], writes=[Be[key]])

        def st_Ln(s, t):
            T_ = TILES[t]
            key = (s, t % 2)
            if T_["first"]:
                S.op("dve", lambda e, s=s: e.memset(fac[(s, 0)], 0.0), writes=[Bfac[(s, 0)]])
                S.op("dve", lambda e, s=s: e.memset(fac[(s, 1)], 0.0), writes=[Bfac[(s, 1)]])
                fcur[s] = 0
            S.op("act", lambda e, key=key, T_=T_: e.activation(sp_t[key][:, 0:T_["N"]], e_t[key][:, 0:T_["N"]], AF.Ln, bias=1.0),
                 reads=[Be[key]], writes=[Bsp[key]])

        def st_C(s, t):
            T_ = TILES[t]
            key = (s, t % 2)
            fc = fcur[s]
            pC = bank(2 + s)

            def mmc(e, pC=pC, key=key, T_=T_, fc=fc, s=s):
                N, r0 = T_["N"], T_["r0"]
                ins = e.matmul(pC[:, 0:N], negU, sp_t[key][:, 0:N], start=True, stop=T_["first"])
                if not T_["first"]:
                    ins = e.matmul(pC[:, 0:N], negOnes, fac[(s, fc)][:, r0:r0 + N], start=False, stop=True)
                return ins
            S.op("pe", mmc, reads=[Bsp[key], Bfac[(s, fc)], Bconst], writes=[PSB[2 + s]])

        def st_fac(s, t):
            T_ = TILES[t]
            if T_["last"]:
                return
            key = (s, t % 2)
            fc = fcur[s]
            N, r0 = T_["N"], T_["r0"]
            S.op("dve", lambda e, key=key, fc=fc, N=N, r0=r0, s=s: e.tensor_tensor(
                fac[(s, 1 - fc)][:, r0:r0 + N], fac[(s, fc)][:, r0:r0 + N], sp_t[key][:, 0:N], ALU.add),
                reads=[Bfac[(s, fc)], Bsp[key]], writes=[Bfac[(s, 1 - fc)]])
            fcur[s] = 1 - fc

        def st_ExpC(s, t):
            T_ = TILES[t]
            key = (s, t % 2)
            S.op("act", lambda e, key=key, T_=T_, s=s: e.activation(ec_t[key][:, 0:T_["N"]], bank(2 + s)[:, 0:T_["N"]], AF.Exp),
                 reads=[PSB[2 + s]], writes=[Bec[key]])

        def st_w(s, t):
            T_ = TILES[t]
            key = (s, t % 2)
            N = T_["N"]
            S.op("dve", lambda e, key=key, N=N: e.tensor_tensor(w_t[key][:, 0:N], e_t[key][:, 0:N], ec_t[key][:, 0:N], ALU.mult),
                 reads=[Be[key], Bec[key]], writes=[Bw[key]])

        def st_PV(s, t):
            T_ = TILES[t]
            key = (s, t % 2)
            pO = bank(4 + s)
            h = heads[s]
            if T_["first"]:
                S.op("pe", lambda e, pO=pO: e.matmul(pO, zero_row[0:1, 0:128], zero_row[0:1, :], start=True, stop=False),
                     reads=[Bconst], writes=[PSB[4 + s]])
            S.op("pe", lambda e, pO=pO, key=key, T_=T_, s=s: e.matmul(
                pO[:, T_["r0"]:T_["r0"] + T_["N"]], Vp[kvb][:, T_["p_own"] * 8 + T_["ib"], s * 128:(s + 1) * 128],
                w_t[key][:, 0:T_["N"]], start=False, stop=T_["last"], skip_group_check=True),
                reads=[BVp[kvb], Bw[key]], writes=[PSB[4 + s]])
            if T_["last"]:
                gq = T_["gq"]
                S.op("dve", lambda e, pO=pO, gq=gq, h=h: e.tensor_tensor(
                    sZT[:, h, gq * 512:(gq + 1) * 512], pO, sZT[:, h, gq * 512:(gq + 1) * 512], ALU.mult),
                    reads=[PSB[4 + s], BsZ[h]], writes=[BsZ[h]])

        def ok(t):
            return 0 <= t < NTL
        for k in range(-3, NTL):
            for s in range(2):
                if ok(k + 1):
                    st_Ln(s, k + 1)
            for s in range(2):
                if ok(k):
                    st_PV(s, k)
            for s in range(2):
                if ok(k + 1):
                    st_C(s, k + 1)
            for s in range(2):
                if ok(k + 1):
                    st_fac(s, k + 1)
            for s in range(2):
                if ok(k + 2):
                    st_ExpL(s, k + 2)
            for s in range(2):
                if ok(k + 2):
                    st_mask(s, k + 2)
            for s in range(2):
                if ok(k + 3):
                    st_L(s, k + 3)
            for s in range(2):
                if ok(k + 1):
                    st_ExpC(s, k + 1)
            for s in range(2):
                if ok(k + 1):
                    st_w(s, k + 1)
            next(sg, None)
        for _ in sg:
            pass

    ck_load(0)
    cv_load(0)
    cv_load(1)
    kv_load(0)
    kv_load(1)
    for pi in range(8):
        emit_pair(pi)
        if pi + 2 < 8:
            kv_load(pi + 2)


    if stage == 5:
        S.build()
        return nc
    R_temp.flip()
    R_hi.flip()
    for k, v in R_temp.hist.items():
        if R_hi.hist.get(k, 0) < v:
            R_hi.hist[k] = v
    x2 = view(OFF_HI, 73728, F32, "p (t d) -> p t d", t=9)
    Bx2 = [R_hi.new("x2_%d" % i) for i in range(NT)]
    for b in BQT:
        for k, v in b.alldeps().items():
            if R_vn.hist.get(k, 0) < v:
                R_vn.hist[k] = v
    x1c = [view(OFF_VN + i * 2048, 2048, F32) for i in range(4)]
    gbc2 = view(OFF_VN + 8192, 8192, F32)
    ystg = [view(OFF_VN + 16384 + i * 8192, 8192, F32) for i in range(2)]
    Bx1c = [Buf("x1c%d" % i, R_vn.hist) for i in range(4)]
    Bgbc2 = Buf("gbc2", R_vn.hist)
    Bystg = [Buf("ystg%d" % i, R_vn.hist) for i in range(2)]
    S.dma("sp", "gbc", gbc2, final_g.broadcast_to([128, D]), writes=[Bgbc2])
    items = [(dk, tt) for dk in range(4) for tt in range(NT)]

    def x1c_load(i):
        dk, tt = items[i]
        pt = PT[tt]
        S.dma("sp", "xc%d" % (i % 4), x1c[i % 4][:pt, :], x1_scr[tt * 128:tt * 128 + pt, dk * 512:(dk + 1) * 512],
              reads=[Bx1scr[tt]], writes=[Bx1c[i % 4]])
    x1c_load(0)
    x1c_load(1)
    for i, (dk, tt) in enumerate(items):
        pt = PT[tt]
        if tt == 0:
            cur_s = wnext()
        wv = wslot(cur_s, "p (a n) -> p a n", a=16)
        if i + 2 < len(items):
            x1c_load(i + 2)
        pb = i % 4
        po = bank(pb)

        def mmo(e, wv=wv, po=po, tt=tt, pt=pt):
            ins = None
            for hh in range(16):
                ins = e.matmul(po[:pt, :], sZT[:, hh, tt * 128:tt * 128 + pt], wv[:, hh, :],
                               start=(hh == 0), stop=(hh == 15))
            return ins
        S.op("pe", mmo, reads=BsZ + [WB[cur_s]], writes=[PSB[pb]])
        xd = x2[:pt, tt, dk * 512:(dk + 1) * 512]
        S.op("dve", lambda e, xd=xd, po=po, i=i, pt=pt: e.tensor_tensor(xd, po[:pt, :], x1c[i % 4][:pt, :], ALU.add),
             reads=[PSB[pb], Bx1c[i % 4]], writes=[Bx2[tt]])
        if dk == 3:
            ys = ystg[tt % 2]
            S.op("act", lambda e, ys=ys, tt=tt, pt=pt: e.activation(ys[:pt, :], x2[:pt, tt, :], AF.Square,
                                                                   accum_out=ssq[:pt, 18 + tt:19 + tt]),
                 reads=[Bx2[tt], Bconst], writes=[Bystg[tt % 2], Bstat[tt]])
            c = 18 + tt
            S.op("act", lambda e, c=c, pt=pt: e.activation(rsd[:pt, c:c + 1], ssq[:pt, c:c + 1], AF.Sqrt, bias=1e-6,
                                                           scale=1.0 / D), reads=[Bstat[tt]], writes=[Bstat[tt]])
            S.op("dve", lambda e, c=c, pt=pt: e.reciprocal(rsd[:pt, c:c + 1], rsd[:pt, c:c + 1]),
                 reads=[Bstat[tt]], writes=[Bstat[tt]])
            S.op("dve", lambda e, ys=ys, tt=tt, c=c, pt=pt: e.scalar_tensor_tensor(
                ys[:pt, :], x2[:pt, tt, :], rsd[:pt, c:c + 1], gbc2[:pt, :], ALU.mult, ALU.mult),
                reads=[Bx2[tt], Bstat[tt], Bgbc2], writes=[Bystg[tt % 2]])
            dst = yp_o[tt * 128:(tt + 1) * 128, :] if tt < 8 else ys_o[:, :]
            S.dma("sp", "ystg%d" % (tt % 2), dst, ys[:pt, :], reads=[Bystg[tt % 2]], final=True)

    S.build()
    return nc


_PROG = {}


def _masks_for(p):
    s = np.arange(128)[:, None]
    t = np.arange(128)[None, :]
    tri = (s < t).astype(np.float32)
    ones = np.ones((128, 128), np.float32)
    zeros = np.zeros((128, 128), np.float32)
    m = np.zeros((5, 128, 128), np.float32)
    for ipar in range(2):
        is_E = (G_BLOCKS[p][ipar] == 2 * ipar)
        m[ipar * 2 + 0] = tri if is_E else ones
        m[ipar * 2 + 1] = zeros if is_E else tri
    m[4] = tri
    return m


def kernel(x_prompt, x_sample, cache_sb_k, cache_sb_v, norm_g, final_norm_g,
           gm_w_in, gm_ln_g, gm_ln_b, gm_w_s, gm_b_s, gm_w_out, sb_w_in, sb_w_out, _stage=2):
    f = lambda a: np.ascontiguousarray(np.asarray(a, dtype=np.float32))
    x_prompt, x_sample, cache_sb_k, cache_sb_v = f(x_prompt), f(x_sample), f(cache_sb_k), f(cache_sb_v)
    if _stage not in _PROG:
        _PROG[_stage] = build_program(_stage)
    nc = _PROG[_stage]
    shared = {
        "norm_g": f(norm_g), "final_g": f(final_norm_g).reshape(1, D),
        "gm_w_in": f(gm_w_in).reshape(D, 3 * GW), "gm_ln_g": f(gm_ln_g).reshape(1, GW),
        "gm_ln_b": f(gm_ln_b).reshape(1, GW), "gm_w_s": f(gm_w_s).reshape(16 * 128, 128),
        "gm_b_s": f(gm_b_s).reshape(1, 16 * 128), "gm_w_out": f(gm_w_out).reshape(GW, D),
        "sb_w_in": f(sb_w_in).reshape(D, 4 * D), "sb_w_out": f(sb_w_out).reshape(D, D),
    }
    in_maps = []
    for c in range(NCORES):
        b, p = c // 2, c % 2
        blocks = x_prompt[b].reshape(16, 128, D)[G_BLOCKS[p]].reshape(1024, D)
        m = dict(shared)
        m["xp"] = np.ascontiguousarray(blocks)
        m["xs"] = np.ascontiguousarray(x_sample[c])
        m["ck"] = np.ascontiguousarray(cache_sb_k[0, c].reshape(1024, D))
        m["cv"] = np.ascontiguousarray(cache_sb_v[0, c].reshape(1024, D))
        m["masks"] = _masks_for(p)
        in_maps.append(m)
    res = run_bass_kernel_spmd(nc, in_maps, core_ids=list(range(NCORES)))
    y_prompt = np.zeros((4, 2048, D), np.float32)
    y_sample = np.zeros((8, 32, D), np.float32)
    k_p = np.zeros((1, 4, 2048, 16, 128), np.float32)
    v_p = np.zeros((1, 4, 2048, 16, 128), np.float32)
    k_s = np.zeros((1, 8, 32, 16, 128), np.float32)
    v_s = np.zeros((1, 8, 32, 16, 128), np.float32)
    gmv = np.zeros((1, 8, 32, GW), np.float32)
    for c in range(NCORES):
        b, p = c // 2, c % 2
        r = res.results[c]
        for i, gblk in enumerate(G_BLOCKS[p]):
            sl = slice(gblk * 128, (gblk + 1) * 128)
            y_prompt[b, sl] = r["yp"][i * 128:(i + 1) * 128]
            k_p[0, b, sl] = r["kp"][i * 128:(i + 1) * 128].reshape(128, 16, 128)
            v_p[0, b, sl] = r["vp"][i * 128:(i + 1) * 128].reshape(128, 16, 128)
        y_sample[c] = r["ys"]
        k_s[0, c] = r["ks"].reshape(32, 16, 128)
        v_s[0, c] = r["vs"].reshape(32, 16, 128)
        gmv[0, c] = r["gmv"]
    return (y_prompt, y_sample, k_p, v_p, k_s, v_s, gmv)
```

```python
import numpy as np
import concourse.bass as bass
import concourse.mybir as mybir
from concourse.bass_utils import run_bass_kernel_spmd

F32 = mybir.dt.float32
BF16 = mybir.dt.bfloat16
F16 = mybir.dt.float16
AF = mybir.ActivationFunctionType
ALU = mybir.AluOpType
AX = mybir.AxisListType

D = 2048
T = 1056
NT = 9
PT = [128] * 8 + [32]
GW = 4096
NCORES = 8
G_BLOCKS = ([0, 3, 4, 7, 8, 11, 12, 15], [1, 2, 5, 6, 9, 10, 13, 14])
SCALE = 128.0 ** -0.5

OFF_CONST = 0
CONST_SZ = 6784
OFF_VN = OFF_CONST + CONST_SZ
VN_SZ = 73728
OFF_BIG = OFF_VN + VN_SZ
BIG_SZ = 99328
OFF_TEMP = OFF_BIG + BIG_SZ
TEMP_SZ = 32768
ARENA_BYTES = OFF_TEMP + TEMP_SZ
OFF_HI = OFF_BIG + 49152


WAR_SAME = True
class Buf:
    __slots__ = ("name", "w", "r", "psum")

    def __init__(self, name, init=None, psum=False):
        self.name = name
        self.w = None
        self.r = dict(init) if init else {}
        self.psum = psum

    def alldeps(self):
        d = dict(self.r)
        if self.w is not None:
            k, v = self.w
            if d.get(k, 0) < v:
                d[k] = v
        return d


class Region:
    def __init__(self, name):
        self.name = name
        self.hist = {}
        self.live = []

    def flip(self):
        for b in self.live:
            for k, v in b.alldeps().items():
                if self.hist.get(k, 0) < v:
                    self.hist[k] = v
        self.live = []

    def new(self, name):
        b = Buf(name, self.hist)
        self.live.append(b)
        return b


class Sched:
    ENGS = ("pe", "act", "dve", "pool", "sp")

    def __init__(self, nc):
        self.nc = nc
        self.streams = {e: [] for e in self.ENGS}
        self.sem = {}
        self.cnt = {}
        self.waited = {e: {} for e in self.ENGS}
        for e in self.ENGS:
            self._mksem("tl_" + e)
        self.out_deps = []

    def _mksem(self, key):
        h = self.nc.alloc_semaphore(name=key)
        self.sem[key] = h
        self.cnt[key] = 0
        return h

    def _deps(self, eng, reads, writes):
        deps = {}
        own = "tl_" + eng

        def add(k, v, war=False):
            if k == own and (eng == "pe" or (war and not WAR_SAME)):
                return
            if deps.get(k, 0) < v:
                deps[k] = v
        for b in reads:
            if b.psum:
                if b.w is not None and b.w[0] != own:
                    add(b.w[0], b.w[1])
                for k, v in b.r.items():
                    if k != own:
                        add(k, v)
            elif b.w is not None:
                add(*b.w)
        for b in writes:
            if b.w is not None and not (b.psum and b.w[0] == own):
                add(b.w[0], b.w[1])
            for k, v in b.r.items():
                if not (b.psum and k == own):
                    add(k, v, war=True)
        out = []
        wd = self.waited[eng]
        for k, v in deps.items():
            if wd.get(k, 0) < v:
                wd[k] = v
                out.append((k, v))
        return out

    def _commit(self, dep, reads, writes):
        for b in writes:
            b.w = dep
            b.r = {}
        k, v = dep
        for b in reads:
            if b not in writes:
                if b.psum:
                    b.w = dep
                    b.r = {}
                elif b.r.get(k, 0) < v:
                    b.r[k] = v

    def op(self, eng, fn, reads=(), writes=()):
        waits = self._deps(eng, reads, writes)
        key = "tl_" + eng
        self.cnt[key] += 1
        dep = (key, self.cnt[key])
        sems = self.sem

        def item(e, waits=waits, fn=fn, key=key):
            for (k, v) in waits:
                e.wait_ge(sems[k], v)
            ins = fn(e)
            ins.then_inc(sems[key], 1)
        self.streams[eng].append(item)
        self._commit(dep, reads, writes)
        return dep

    def dma(self, q, semkey, out, in_, reads=(), writes=(), final=False):
        if semkey not in self.sem:
            self._mksem(semkey)
        waits = self._deps(q, reads, writes)
        self.cnt[semkey] += 16
        dep = (semkey, self.cnt[semkey])
        sems = self.sem

        def item(e, waits=waits, out=out, in_=in_, semkey=semkey):
            for (k, v) in waits:
                e.wait_ge(sems[k], v)
            e.dma_start(out=out, in_=in_).then_inc(sems[semkey], 16)
        self.streams[q].append(item)
        self._commit(dep, reads, writes)
        if final:
            self.out_deps.append(dep)
        return dep

    def custom(self, eng, semkey, inc, fn, reads=(), writes=()):
        if semkey not in self.sem:
            self._mksem(semkey)
        waits = self._deps(eng, reads, writes)
        self.cnt[semkey] += inc
        dep = (semkey, self.cnt[semkey])
        sems = self.sem

        def item(e, waits=waits, fn=fn, semkey=semkey, inc=inc):
            for (k, v) in waits:
                e.wait_ge(sems[k], v)
            fn(e).then_inc(sems[semkey], inc)
        self.streams[eng].append(item)
        self._commit(dep, reads, writes)
        return dep

    def build(self):
        nc = self.nc
        fin = {}
        for (k, v) in self.out_deps:
            if fin.get(k, 0) < v:
                fin[k] = v
        sems = self.sem

        def fin_item(e):
            for k, v in fin.items():
                e.wait_ge(sems[k], v)
        self.streams["sp"].append(fin_item)
        streams = self.streams
        with nc.Block() as block:
            @block.tensor
            def _(e):
                for it in streams["pe"]:
                    it(e)

            @block.scalar
            def _(e):
                for it in streams["act"]:
                    it(e)

            @block.vector
            def _(e):
                for it in streams["dve"]:
                    it(e)

            @block.gpsimd
            def _(e):
                for it in streams["pool"]:
                    it(e)

            @block.sync
            def _(e):
                for it in streams["sp"]:
                    it(e)


def build_program(stage=2):
    nc = bass.Bass("TRN2", target_bir_lowering=False)

    def din(name, shape, dt=F32):
        return nc.dram_tensor(name, list(shape), dt, kind="ExternalInput").ap()

    def dout(name, shape, dt=F32):
        return nc.dram_tensor(name, list(shape), dt, kind="ExternalOutput").ap()

    xp = din("xp", [1024, D])
    xs = din("xs", [32, D])
    ck = din("ck", [1024, D])
    cv = din("cv", [1024, D])
    masks_in = din("masks", [5, 128, 128])
    norm_g = din("norm_g", [2, D])
    final_g = din("final_g", [1, D])
    gm_w_in = din("gm_w_in", [D, 3 * GW])
    gm_ln_g = din("gm_ln_g", [1, GW])
    gm_ln_b = din("gm_ln_b", [1, GW])
    gm_w_s = din("gm_w_s", [16 * 128, 128])
    gm_b_s = din("gm_b_s", [1, 16 * 128])
    gm_w_out = din("gm_w_out", [GW, D])
    sb_w_in = din("sb_w_in", [D, 4 * D])
    sb_w_out = din("sb_w_out", [D, D])

    yp_o = dout("yp", [1024, D])
    ys_o = dout("ys", [32, D])
    kp_o = dout("kp", [1024, D])
    vp_o = dout("vp", [1024, D])
    ks_o = dout("ks", [32, D])
    vs_o = dout("vs", [32, D])
    gmv_o = dout("gmv", [32, GW])

    x1_scr = nc.dram_tensor("x1_scr", [T, D], F32).ap()
    kt_send = [nc.dram_tensor("kt_send%d" % i, [512, 1024], BF16) for i in range(4)]
    kt_all = [nc.dram_tensor("kt_all%d" % i, [1024, 1024], BF16) for i in range(4)]
    v_send = [nc.dram_tensor("v_send%d" % i, [1024, 512], BF16) for i in range(4)]
    v_all = [nc.dram_tensor("v_all%d" % i, [2048, 512], BF16) for i in range(4)]

    arena = nc.alloc_sbuf_tensor("arena", [128, ARENA_BYTES // 2], BF16)
    pall = nc.alloc_psum_tensor("pall", [128, 4096], F32)

    def view(off, nbytes, dt=BF16, pat=None, **kw):
        assert off % 4 == 0 and off + nbytes <= ARENA_BYTES, (off, nbytes)
        a = arena[:, off // 2:(off + nbytes) // 2]
        if dt != BF16:
            a = a.bitcast(dt)
        if pat:
            a = a.rearrange(pat, **kw)
        return a

    def bank(b, n=1):
        return pall[:, b * 512:(b + n) * 512]

    S = Sched(nc)
    PSB = [Buf("ps%d" % i, psum=True) for i in range(8)]

    ident = view(0, 256)
    negU = view(256, 256)
    negOnes = view(512, 256)
    ones_row = view(768, 512, F32)
    zero_row = view(1280, 1024)
    masks = view(2304, 2560, F32, "p (m t) -> p m t", m=5)
    ssq = view(4864, 256, F32)
    rsd = view(5120, 256, F32)
    s1 = view(5376, 288, F32, "p (t w) -> p t w", t=9)
    s2 = view(5664, 288, F32, "p (t w) -> p t w", t=9)
    mv = view(5952, 288, F32, "p (t w) -> p t w", t=9)
    fss = view(6240, 144, F32, "p (t w) -> p t w", t=9)
    s12 = view(5376, 576, F32, "p (k t w) -> p k t w", k=2, t=9)
    lnscr = view(6400, 64, F32)
    Blnscr = Buf("lnscr")
    Bconst = Buf("const")
    Bstat = [Buf("stat%d" % i) for i in range(NT)]

    R_temp = Region("temp")
    R_hi = Region("hi")
    R_vn = Region("vn")

    tmpf = view(OFF_TEMP, 512, F32)
    Btmpf = R_temp.new("tmpf")
    S.op("dve", lambda e: e.memset(tmpf, 1.0), writes=[Btmpf])
    S.op("pool", lambda e: e.affine_select(out=tmpf, in_=tmpf, pattern=[[-1, 128]], compare_op=ALU.is_equal,
                                           fill=0.0, base=0, channel_multiplier=1), reads=[Btmpf], writes=[Btmpf])
    S.op("dve", lambda e: e.tensor_copy(ident, tmpf), reads=[Btmpf], writes=[Bconst])
    S.op("dve", lambda e: e.memset(tmpf, -1.0), reads=[], writes=[Btmpf])
    S.op("pool", lambda e: e.affine_select(out=tmpf, in_=tmpf, pattern=[[-1, 128]], compare_op=ALU.is_ge,
                                           fill=0.0, base=0, channel_multiplier=1), reads=[Btmpf], writes=[Btmpf])
    S.op("dve", lambda e: e.tensor_copy(negU, tmpf), reads=[Btmpf, Bconst], writes=[Bconst])
    S.op("dve", lambda e: e.memset(negOnes, -1.0), reads=[Bconst], writes=[Bconst])
    S.op("dve", lambda e: e.memset(ones_row, 1.0), reads=[Bconst], writes=[Bconst])
    S.op("dve", lambda e: e.memset(zero_row[0:1, :], 0.0), reads=[Bconst], writes=[Bconst])
    S.op("dve", lambda e: e.memset(ssq, 0.0), reads=[Bconst], writes=[Bconst])
    S.op("dve", lambda e: e.memset(lnscr[:, 12:13], -0.5), reads=[Bconst], writes=[Bconst])
    S.op("dve", lambda e: e.memset(s1, 0.0), reads=[Bconst], writes=[Bconst])
    S.op("dve", lambda e: e.memset(s2, 0.0), reads=[Bconst], writes=[Bconst])
    S.op("dve", lambda e: e.memset(fss, 0.0), reads=[Bconst], writes=[Bconst])
    Bmask = Buf("masks")
    S.dma("sp", "cst", masks, masks_in.rearrange("m s t -> s m t"), writes=[Bmask])

    WB = [Buf("w%d" % i) for i in range(3)]

    def wslot(s, pat, **kw):
        return view(OFF_BIG + s * 16384, 16384, BF16, pat, **kw)

    wchunks = []

    def wv_chunk(wmat, c0):
        return [(lambda s: wslot(s, "p (a n) -> p a n", a=16),
                 wmat[:, c0:c0 + 512].rearrange("(a p) n -> p a n", p=128))]

    def wpair_chunk(wmat, c0, c1):
        return [(lambda s: wslot(s, "p (a n) -> p a n", a=16)[:, :, 0:256],
                 wmat[:, c0:c0 + 256].rearrange("(a p) n -> p a n", p=128)),
                (lambda s: wslot(s, "p (a n) -> p a n", a=16)[:, :, 256:512],
                 wmat[:, c1:c1 + 256].rearrange("(a p) n -> p a n", p=128))]

    def wout0_chunk(c0):
        return [(lambda s: wslot(s, "p (a n) -> p a n", a=32),
                 gm_w_out[:, c0:c0 + 256].rearrange("(a p) n -> p a n", p=128))]

    for wc in range(8):
        wchunks.append(wv_chunk(gm_w_in, GW + wc * 512))
    for j in range(16):
        wchunks.append(wpair_chunk(gm_w_in, j * 256, 2 * GW + j * 256))
    for dk in range(8):
        wchunks.append(wout0_chunk(dk * 256))
    if stage >= 2:
        L1_ORDER = [("k", 0), ("q", 0), ("k", 1), ("q", 1), ("k", 2), ("q", 2), ("k", 3), ("q", 3),
                    ("v", 0), ("q", 4), ("v", 1), ("q", 5), ("v", 2), ("q", 6), ("v", 3), ("q", 7)]
        for (kind, idx) in L1_ORDER:
            if kind == "k":
                wchunks.append(wv_chunk(sb_w_in, D + idx * 512))
            elif kind == "v":
                wchunks.append(wv_chunk(sb_w_in, 2 * D + idx * 512))
            else:
                wchunks.append(wpair_chunk(sb_w_in, idx * 256, 3 * D + idx * 256))
        for dk in range(4):
            wchunks.append(wv_chunk(sb_w_out, dk * 512))
    wstate = {"issued": 0, "used": 0}

    def wissue_upto(n):
        while wstate["issued"] <= min(n, len(wchunks) - 1):
            i = wstate["issued"]
            s = i % 3
            for (vf, src) in wchunks[i]:
                S.dma("pool", "w%d" % s, vf(s), src, writes=[WB[s]])
            wstate["issued"] += 1

    def wnext():
        i = wstate["used"]
        wissue_upto(i + 2)
        wstate["used"] += 1
        return i % 3

    wissue_upto(1)

    def norm_setup(src_rows, g_src, dstT, BdstT, ss_col0, src_bufs, pbase, dve_stats=True, split=False):
        R_temp.flip()
        xsl = [view(OFF_TEMP + i * 8192, 8192, F32) for i in range(2)]
        hsl = [view(OFF_TEMP + 16384 + i * 4096, 4096) for i in range(2)]
        gbc = view(OFF_TEMP + 24576, 8192, F32)
        Bx = [R_temp.new("x%d" % i) for i in range(2)]
        Bh = [R_temp.new("h%d" % i) for i in range(2)]
        Bg = R_temp.new("gbc")
        S.dma("sp", "gbc", gbc, g_src.broadcast_to([128, D]), writes=[Bg])

        loaded = set()
        pre_done = set()

        def load(tt):
            pt = PT[tt]
            loaded.add(tt)
            S.dma("sp", "xin%d" % (tt % 2), xsl[tt % 2][:pt, :], src_rows(tt),
                  reads=[src_bufs[tt]] if src_bufs else [], writes=[Bx[tt % 2]])

        def pre(tt):
            pt = PT[tt]
            pre_done.add(tt)
            xv = xsl[tt % 2][:pt, :]
            hv = hsl[tt % 2][:pt, :]
            c = ss_col0 + tt
            if not dve_stats:
                S.op("act", lambda e, hv=hv, xv=xv, c=c, pt=pt: e.activation(hv, xv, AF.Square, accum_out=ssq[:pt, c:c + 1]),
                     reads=[Bx[tt % 2], Bconst], writes=[Bh[tt % 2], Bstat[tt]])
                S.op("act", lambda e, c=c, pt=pt: e.activation(rsd[:pt, c:c + 1], ssq[:pt, c:c + 1], AF.Sqrt, bias=1e-6,
                                                               scale=1.0 / D), reads=[Bstat[tt]], writes=[Bstat[tt]])
                S.op("dve", lambda e, c=c, pt=pt: e.reciprocal(rsd[:pt, c:c + 1], rsd[:pt, c:c + 1]),
                     reads=[Bstat[tt]], writes=[Bstat[tt]])
                return
            S.op("dve", lambda e, hv=hv, xv=xv, c=c, pt=pt: e.scalar_tensor_tensor(
                hv, xv, 1.0, xv, ALU.mult, ALU.mult, accum_out=ssq[:pt, c:c + 1]),
                reads=[Bx[tt % 2], Bconst], writes=[Bh[tt % 2], Bstat[tt]])
            S.op("dve", lambda e, c=c, pt=pt: e.tensor_scalar(rsd[:pt, c:c + 1], ssq[:pt, c:c + 1], 1.0 / D, 1e-6,
                                                              ALU.mult, ALU.add), reads=[Bstat[tt]], writes=[Bstat[tt]])
            S.op("pool", lambda e, c=c, pt=pt: e.tensor_tensor(rsd[:pt, c:c + 1], rsd[:pt, c:c + 1], lnscr[:pt, 12:13],
                                                               ALU.pow), reads=[Bstat[tt], Bconst], writes=[Bstat[tt]])

        def tile(tt):
            pt = PT[tt]
            xv = xsl[tt % 2][:pt, :]
            hv = hsl[tt % 2][:pt, :]
            c = ss_col0 + tt
            if tt not in pre_done:
                pre(tt)
            if dve_stats and tt + 1 < NT and (tt + 1) in loaded and (tt + 1) not in pre_done:
                pre(tt + 1)
            S.op("dve", lambda e, hv=hv, xv=xv, c=c, pt=pt: e.scalar_tensor_tensor(
                hv, xv, rsd[:pt, c:c + 1], gbc[:pt, :], ALU.mult, ALU.mult),
                reads=[Bx[tt % 2], Bstat[tt], Bg], writes=[Bh[tt % 2]])
            if not split:
                back(tt)

        def back(tt):
            pt = PT[tt]
            hv = hsl[tt % 2][:pt, :]
            for half in range(2):
                pb = pbase + half
                pv = bank(pb).bitcast(BF16).rearrange("p (a t) -> p a t", a=8)

                def trs(e, hv=hv, pv=pv, half=half, pt=pt):
                    ins = None
                    for j in range(8):
                        dc = half * 8 + j
                        ins = e.transpose(pv[:, j, :pt], hv[:, dc * 128:(dc + 1) * 128], ident[:pt, :pt])
                    return ins
                S.op("pe", trs, reads=[Bh[tt % 2], Bconst], writes=[PSB[pb]])
                dv = dstT[:, half * 8:(half + 1) * 8, tt * 128:tt * 128 + pt]
                if half == 0:
                    S.op("act", lambda e, dv=dv, pv=pv, pt=pt: e.copy(dv, pv[:, :, :pt]), reads=[PSB[pb]], writes=[BdstT[tt]])
                else:
                    S.op("dve", lambda e, dv=dv, pv=pv, pt=pt: e.tensor_copy(dv, pv[:, :, :pt]), reads=[PSB[pb]],
                         writes=[BdstT[tt]])
        if split:
            return load, tile, back
        return load, tile

    hT = view(OFF_HI, 33792, BF16, "p (a t) -> p a t", a=16)
    BhT = [R_hi.new("hT%d" % i) for i in range(NT)]

    def x_rows(tt):
        return xp[tt * 128:(tt + 1) * 128, :] if tt < 8 else xs[:, :]
    a_load, a_tile = norm_setup(x_rows, norm_g[0:1, :], hT, BhT, 0, None, 0)

    lng = view(OFF_TEMP, 16384, F32)
    OFF_SP = OFF_BIG + 82944
    t1s = [view(OFF_SP + i * 4096, 4096, F32, "p (a k) -> p a k", a=8) for i in range(2)]
    sstg = view(OFF_SP + 8192, 2048, F32, "p (a k) -> p a k", a=4)
    lnb32 = view(OFF_SP + 10240, 2048, F32, "p (a k) -> p a k", a=4)
    junk = view(OFF_SP + 8192, 1024, BF16, "p (a k) -> p a k", a=4)
    wst = view(OFF_SP + 12288, 4096, BF16, "p (g s) -> p g s", g=16)
    Bt1 = [R_hi.new("t1_%d" % i) for i in range(2)]
    Bsstg = R_hi.new("sstg")
    Blnb32 = R_hi.new("lnb32")
    Bjunk = Bsstg
    Bwst = R_hi.new("wst")
    vn16 = view(OFF_VN, VN_SZ, F16, "p (c t k) -> p c t k", c=32, t=9)
    vnb = view(OFF_VN, VN_SZ, BF16, "p (c t k) -> p c t k", c=32, t=9)
    Bvn = [[R_vn.new("vn%d_%d" % (cc, tt)) for tt in range(NT)] for cc in range(32)]
    Bvn_cc = None

    def ln_stats(tt):
        pt = PT[tt]
        st = Bstat[tt]
        scrA = lnscr[:pt, 0:8].rearrange("p (k w) -> p k w", k=2)
        scrB = lnscr[:pt, 8:12].rearrange("p (k w) -> p k w", k=2)
        S.op("dve", lambda e: e.tensor_tensor(scrA, s12[:pt, :, tt, 0:4], s12[:pt, :, tt, 4:8], ALU.add),
             reads=[st], writes=[Blnscr])
        S.op("dve", lambda e: e.tensor_tensor(scrB, scrA[:, :, 0:2], scrA[:, :, 2:4], ALU.add),
             reads=[Blnscr], writes=[Blnscr])
        S.op("dve", lambda e: e.tensor_tensor(mv[:pt, tt, 0:2].rearrange("p (k o) -> p k o", o=1), scrB[:, :, 0:1],
                                               scrB[:, :, 1:2], ALU.add), reads=[Blnscr, st], writes=[st])
        S.op("dve", lambda e: e.tensor_scalar(mv[:pt, tt, 2:3], mv[:pt, tt, 0:1], 1.0 / GW, None, ALU.mult),
             reads=[st], writes=[st])
        S.op("dve", lambda e: e.tensor_tensor(mv[:pt, tt, 3:4], mv[:pt, tt, 2:3], mv[:pt, tt, 2:3], ALU.mult),
             reads=[st], writes=[st])
        S.op("dve", lambda e: e.tensor_scalar(mv[:pt, tt, 6:7], mv[:pt, tt, 1:2], 1.0 / GW, None, ALU.mult),
             reads=[st], writes=[st])
        S.op("dve", lambda e: e.tensor_tensor(mv[:pt, tt, 4:5], mv[:pt, tt, 6:7], mv[:pt, tt, 3:4], ALU.subtract),
             reads=[st], writes=[st])
        S.op("dve", lambda e: e.tensor_scalar(mv[:pt, tt, 7:8], mv[:pt, tt, 4:5], 1e-5, None, ALU.add),
             reads=[st], writes=[st])
        S.op("pool", lambda e: e.tensor_tensor(mv[:pt, tt, 5:6], mv[:pt, tt, 7:8], lnscr[:pt, 12:13], ALU.pow),
             reads=[st, Bconst], writes=[st])
    def ln_norm(tt):
        pt = PT[tt]
        st = Bstat[tt]
        S.op("dve", lambda e: e.scalar_tensor_tensor(mv[:pt, tt, 3:4], mv[:pt, tt, 2:3], -1.0, mv[:pt, tt, 5:6],
                                                     ALU.mult, ALU.mult), reads=[st], writes=[st])
        if tt < 8:
            for pc in range(4):
                t1 = t1s[pc % 2]
                gsl = lng[:pt, pc * 1024:(pc + 1) * 1024].rearrange("p (a k) -> p a k", a=8)
                src = vn16[:pt, pc * 8:(pc + 1) * 8, tt, :]
                dst = vnb[:pt, pc * 8:(pc + 1) * 8, tt, :]
                vb = [Bvn[cc][tt] for cc in range(pc * 8, pc * 8 + 8)]
                S.op("act", lambda e, t1=t1, src=src: e.activation(t1[:pt], src, AF.Identity, scale=mv[:pt, tt, 5:6],
                                                                   bias=mv[:pt, tt, 3:4]),
                     reads=vb + [st], writes=[Bt1[pc % 2]])
                S.op("dve", lambda e, t1=t1, dst=dst, gsl=gsl: e.tensor_tensor(dst, t1[:pt], gsl, ALU.mult),
                     reads=[Bt1[pc % 2], Blng], writes=vb)
        else:
            for pc in range(8):
                t1 = t1s[pc % 2][:, 0:4, :]
                gsl = lng[:pt, pc * 512:(pc + 1) * 512].rearrange("p (a k) -> p a k", a=4)
                src = vn16[:pt, pc * 4:(pc + 1) * 4, tt, :]
                dst = vnb[:pt, pc * 4:(pc + 1) * 4, tt, :]
                vb = [Bvn[cc][tt] for cc in range(pc * 4, pc * 4 + 4)]
                S.dma("sp", "lnb32", lnb32[:pt], gm_ln_b[:, pc * 512:(pc + 1) * 512].broadcast_to([32, 512])
                      .rearrange("p (a k) -> p a k", a=4), writes=[Blnb32])
                S.op("act", lambda e, t1=t1, src=src: e.activation(t1[:pt], src, AF.Identity, scale=mv[:pt, tt, 5:6],
                                                                   bias=mv[:pt, tt, 3:4]),
                     reads=vb + [st], writes=[Bt1[pc % 2]])
                S.op("dve", lambda e, t1=t1, gsl=gsl: e.tensor_tensor(sstg[:pt], t1[:pt], gsl, ALU.mult),
                     reads=[Bt1[pc % 2], Blng], writes=[Bsstg])
                S.op("act", lambda e, dst=dst: e.copy(dst, sstg[:pt]), reads=[Bsstg], writes=vb)
                S.op("dve", lambda e: e.tensor_tensor(sstg[:pt], sstg[:pt], lnb32[:pt], ALU.add),
                     reads=[Bsstg, Blnb32], writes=[Bsstg])
                S.dma("sp", "gmv", gmv_o[:, pc * 512:(pc + 1) * 512].rearrange("p (a k) -> p a k", a=4),
                      sstg[:pt], reads=[Bsstg], final=True)

    pbk = 0
    Blng = None
    hist_after_A = None
    a_load(0)
    for wc in range(8):
        s = wnext()
        wv = wslot(s, "p (a n) -> p a n", a=16)
        if wc == 6:
            S.dma("pool", "wst", wst, gm_w_s.rearrange("(g t) s -> t g s", g=16), writes=[Bwst])
            S.op("dve", lambda e: e.memset(wst[0:64, :, 64:128], 0.0), reads=[Bwst], writes=[Bwst])
        if wc == 1:
            R_temp.flip()
            hist_after_A = dict(R_temp.hist)
            Blng = R_temp.new("lng")
            S.dma("sp", "lngb", lng, gm_ln_g.broadcast_to([128, GW]), writes=[Blng])
        for tt in range(NT):
            pt = PT[tt]
            if wc == 0:
                if tt + 1 < NT:
                    a_load(tt + 1)
                a_tile(tt)
            pb = 2 + (pbk % 4)
            pbk += 1
            po = bank(pb)

            def mm(e, wv=wv, po=po, tt=tt, pt=pt):
                ins = None
                for dc in range(16):
                    ins = e.matmul(po[:pt, :], hT[:, dc, tt * 128:tt * 128 + pt], wv[:, dc, :],
                                   start=(dc == 0), stop=(dc == 15))
                return ins
            S.op("pe", mm, reads=[BhT[tt], WB[s]], writes=[PSB[pb]])
            vb = [Bvn[cc][tt] for cc in range(wc * 4, wc * 4 + 4)]
            gdst = vn16[:pt, wc * 4:(wc + 1) * 4, tt, :]
            S.op("act", lambda e, gdst=gdst, po=po, pt=pt, tt=tt, wc=wc: e.activation(
                gdst, po[:pt, :].rearrange("p (a k) -> p a k", a=4), AF.Gelu_apprx_tanh,
                accum_out=s1[:pt, tt, wc:wc + 1]), reads=[PSB[pb], Bconst], writes=vb + [Bstat[tt]])
            S.op("dve", lambda e, gdst=gdst, pt=pt, tt=tt, wc=wc: e.scalar_tensor_tensor(
                junk[:pt], gdst, 1.0, gdst, ALU.mult, ALU.mult, accum_out=s2[:pt, tt, wc:wc + 1]),
                reads=vb, writes=[Bjunk, Bstat[tt]])
            if wc == 7:
                ln_stats(tt)
                if tt >= 3:
                    ln_norm(tt - 3)
                if tt == NT - 1:
                    ln_norm(tt - 2)
                    ln_norm(tt - 1)
                    ln_norm(tt)

    gus = [view(OFF_TEMP + 16384 + i * 4224, 2112) for i in range(3)]
    szs = [view(OFF_TEMP + 16384 + i * 4224 + 2112, 2112) for i in range(3)]
    Bgu = [Buf("gu%d" % i, hist_after_A) for i in range(3)]
    Bsz = [Buf("sz%d" % i, hist_after_A) for i in range(3)]
    R_temp.flip()
    R_temp.live.extend(Bgu + Bsz)
    t1c = [view(OFF_TEMP, 4224, F32)] * 2
    rhs2 = view(OFF_TEMP + 4224, 8192, F32)
    rhs2s = view(OFF_TEMP + 12416, 2048, F32)
    lhs2 = [view(OFF_TEMP + 14464 + i * 512, 512, F32) for i in range(2)]
    Bt1c = [R_temp.new("t1c")] * 2
    Brhs2 = R_temp.new("rhs2")
    Brhs2s = R_temp.new("rhs2s")
    Blhs2 = [R_temp.new("lhs2_%d" % i) for i in range(2)]
    sp_init = {}
    for b in (Bt1[0], Bt1[1], Bsstg, Blnb32, Bjunk):
        for k, v in b.alldeps().items():
            if sp_init.get(k, 0) < v:
                sp_init[k] = v
    gus.append(view(OFF_SP, 2112))
    szs.append(view(OFF_SP + 2112, 2112))
    Bgu.append(Buf("gu3", sp_init))
    Bsz.append(Buf("sz3", sp_init))
    wmT = view(OFF_SP + 4224, 4096, BF16, "p (g t) -> p g t", g=16)
    BwmT = Buf("wmT", sp_init)
    S.dma("sp", "bsr", rhs2[1:2, :], gm_b_s, writes=[Brhs2])
    S.dma("sp", "bsr", rhs2s[1:2, 0:512].rearrange("p (g t) -> p g t", g=16),
          gm_b_s.rearrange("p (g t) -> p g t", g=16)[:, :, 0:32], reads=[Brhs2], writes=[Brhs2s])
    for i in range(2):
        S.op("dve", lambda e, i=i: e.memset(lhs2[i][0:2, :], 1.0), writes=[Blhs2[i]])

    def lhs2_load(cc):
        S.dma("sp", "lhs2_%d" % (cc % 2), lhs2[cc % 2][0:1, :], gm_ln_b[:, cc * 128:(cc + 1) * 128], writes=[Blhs2[cc % 2]])

    def emit_wmT():
        for half in range(2):
            pv = bank(6 + half).bitcast(BF16).rearrange("p (a t) -> p a t", a=8)

            def trw(e, pv=pv, half=half):
                ins = None
                for j in range(8):
                    ins = e.transpose(pv[:, j, :], wst[:, half * 8 + j, :], ident)
                return ins
            S.op("pe", trw, reads=[Bwst, Bconst], writes=[PSB[6 + half]])
            S.op("dve", lambda e, pv=pv, half=half: e.tensor_copy(wmT[:, half * 8:(half + 1) * 8, :], pv),
                 reads=[PSB[6 + half]], writes=[BwmT])
            yield
        for q4 in range(4):
            pwq = bank(6 + q4 % 2)
            S.op("pe", lambda e, q4=q4, pwq=pwq: e.matmul(pwq[0:1, :].rearrange("p (g t) -> p g t", g=4), negOnes[:, 0:1],
                                                          wmT[:, q4 * 4:(q4 + 1) * 4, :], start=True, stop=True),
                 reads=[BwmT, Bconst], writes=[PSB[6 + q4 % 2]])
            S.op("act", lambda e, q4=q4, pwq=pwq: e.mul(rhs2[0:1, q4 * 512:(q4 + 1) * 512], pwq[0:1, :], -1.0),
                 reads=[PSB[6 + q4 % 2]], writes=[Brhs2])
            yield
        pw = bank(6)
        S.op("pe", lambda e: e.matmul(pw[0:1, :].rearrange("p (g t) -> p g t", g=16), negOnes[0:32, 0:1],
                                      wmT[0:32, :, 0:32], start=True, stop=True),
             reads=[BwmT, Bconst], writes=[PSB[6]])
        S.op("act", lambda e: e.mul(rhs2s[0:1, 0:512], pw[0:1, :], -1.0),
             reads=[PSB[6]], writes=[Brhs2s])
        yield

    def yT(cc):
        return view(OFF_VN + cc * 2304, 2112)

    psU = bank(0, 3)
    psZ = bank(3, 3)
    psM = bank(6, 2)
    psMs = bank(0)[:, 384:416]
    psU3 = psU.rearrange("p (r n) -> p r n", r=3)[:, :, 0:352]
    psZ3 = psZ.rearrange("p (r n) -> p r n", r=3)[:, :, 0:352]
    NR = [(0, 352), (352, 352), (704, 352)]
    gus4, szs4, Bgu4, Bsz4 = gus, szs, Bgu, Bsz
    NSET = 4
    MIXLAG = 2
    cstate = {"s": None}
    setup_gen = emit_wmT()

    def emit_uz(cc):
        q = cc % 2
        if q == 0:
            cstate["s"] = wnext()
        s_ = cstate["s"]
        wv = wslot(s_, "p (a n) -> p a n", a=16)
        k4 = cc % NSET
        for (which, col0, pst, bst) in ((0, q * 128, psU, [PSB[0], PSB[1], PSB[2]]),
                                        (1, 256 + q * 128, psZ, [PSB[3], PSB[4], PSB[5]])):
            def mmuz(e, wv=wv, col0=col0, pst=pst):
                ins = None
                for r, (n0, n) in enumerate(NR):
                    o = pst[:, r * 512:r * 512 + n]
                    for dc in range(16):
                        ins = e.matmul(o, wv[:, dc, col0:col0 + 128], hT[:, dc, n0:n0 + n],
                                       start=(dc == 0), stop=(dc == 15))
                return ins
            S.op("pe", mmuz, reads=BhT + [WB[s_]], writes=bst)
            next(setup_gen, None)
        gu = gus4[k4]
        sz = szs4[k4]
        S.op("act", lambda e, gu=gu: e.activation(gu.rearrange("p (r n) -> p r n", r=3), psU3, AF.Gelu_apprx_tanh),
             reads=[PSB[0], PSB[1], PSB[2]], writes=[Bgu4[k4]])
        S.op("act", lambda e, sz=sz: e.activation(sz.rearrange("p (r n) -> p r n", r=3), psZ3, AF.Silu),
             reads=[PSB[3], PSB[4], PSB[5]], writes=[Bsz4[k4]])

    def emit_mix(cc):
        g = cc // 2
        k4 = cc % NSET
        k2 = cc % 2

        if cc + 1 < 32:
            lhs2_load(cc + 1)
        l2 = lhs2[cc % 2]

        def mmix(e, cc=cc, g=g, l2=l2):
            ins = None
            for tb in range(8):
                o = psM[:, tb * 128:(tb + 1) * 128]
                e.matmul(o, vnb[:, cc, tb, :], wmT[:, g, :], start=True, stop=False)
                ins = e.matmul(o, l2[0:2, :], rhs2[0:2, g * 128:(g + 1) * 128], start=False, stop=True)
            e.matmul(psMs, vnb[0:32, cc, 8, :], wmT[0:32, g, 0:32], start=True, stop=False)
            ins = e.matmul(psMs, l2[0:2, :], rhs2s[0:2, g * 32:(g + 1) * 32], start=False, stop=True)
            return ins
        S.op("pe", mmix, reads=Bvn[cc] + [BwmT, Brhs2, Brhs2s, Blhs2[cc % 2], Bconst], writes=[PSB[6], PSB[7], PSB[0]])
        gu = gus4[k4]
        sz = szs4[k4]
        tc = t1c[k2]
        S.op("dve", lambda e, tc=tc, gu=gu: e.tensor_tensor(tc[:, 1024:1056], psMs, gu[:, 1024:1056], ALU.mult),
             reads=[PSB[0], Bgu4[k4]], writes=[Bt1c[k2]])
        S.op("dve", lambda e, tc=tc, gu=gu: e.tensor_tensor(tc[:, 0:1024], psM, gu[:, 0:1024], ALU.mult),
             reads=[PSB[6], PSB[7], Bgu4[k4], Bt1c[k2]], writes=[Bt1c[k2]])
        yv = yT(cc)
        S.op("dve", lambda e, tc=tc, sz=sz, yv=yv: e.tensor_tensor(yv, tc, sz, ALU.mult),
             reads=[Bt1c[k2], Bsz4[k4]], writes=Bvn[cc])

    for cc in range(32):
        emit_uz(cc)
        if cc == MIXLAG:
            for _ in setup_gen:
                pass
            lhs2_load(0)
        if cc >= MIXLAG:
            emit_mix(cc - MIXLAG)
    for cc in range(32 - MIXLAG, 32):
        emit_mix(cc)

    R_hi.flip()
    for o in [Bwst, BwmT] + Bgu4[3:] + Bsz4[3:]:
        for k, v in o.alldeps().items():
            if R_hi.hist.get(k, 0) < v:
                R_hi.hist[k] = v
    xcs = [view(OFF_HI + i * 1024, 1024, F32) for i in range(4)]
    x1s = [view(OFF_HI + 4096 + i * 1024, 1024, F32) for i in range(4)]
    Bxc = [R_hi.new("xc%d" % i) for i in range(4)]
    Bx1s = [R_hi.new("x1s%d" % i) for i in range(4)]
    Bx1scr = [Buf("x1scr%d" % i) for i in range(NT)]
    items = [(dk, tt) for dk in range(8) for tt in range(NT)]

    def xc_load(i):
        dk, tt = items[i]
        pt = PT[tt]
        S.dma("sp", "xc%d" % (i % 4), xcs[i % 4][:pt, :], x_rows(tt)[:, dk * 256:(dk + 1) * 256], writes=[Bxc[i % 4]])
    xc_load(0)
    xc_load(1)
    cur_s = None
    h1T = view(OFF_BIG + 65536, 33792, BF16, "p (a t) -> p a t", a=16)
    Bh1T = [R_hi.new("h1T%d" % i) for i in range(NT)]

    def x1_rows(tt):
        return x1_scr[tt * 128:tt * 128 + PT[tt], :]
    b_load = b_tile = b_back = None
    for i, (dk, tt) in enumerate(items):
        pt = PT[tt]
        if tt == 0:
            cur_s = wnext()
        if dk == 7 and tt == 0 and stage != 1:
            b_load, b_tile, b_back = norm_setup(x1_rows, norm_g[1:2, :], h1T, Bh1T, 9, Bx1scr, 4, dve_stats=False, split=True)
        wv = wslot(cur_s, "p (a n) -> p a n", a=32)
        if i + 2 < len(items):
            xc_load(i + 2)
        pb = i % 4
        po = bank(pb)

        def mmo(e, wv=wv, po=po, tt=tt, pt=pt):
            ins = None
            for cc in range(32):
                ins = e.matmul(po[:pt, 0:256], yT(cc)[:, tt * 128:tt * 128 + pt], wv[:, cc, :],
                               start=(cc == 0), stop=(cc == 31))
            return ins
        S.op("pe", mmo, reads=[Bvn[cc][0] for cc in range(32)] + [WB[cur_s]], writes=[PSB[pb]])
        S.op("dve", lambda e, po=po, i=i, pt=pt: e.tensor_tensor(x1s[i % 4][:pt, :], po[:pt, 0:256], xcs[i % 4][:pt, :], ALU.add),
             reads=[PSB[pb], Bxc[i % 4]], writes=[Bx1s[i % 4]])
        S.dma("sp", "x1s%d" % (i % 4), x1_scr[tt * 128:tt * 128 + pt, dk * 256:(dk + 1) * 256], x1s[i % 4][:pt, :],
              reads=[Bx1s[i % 4]], writes=[Bx1scr[tt]])
        if b_load is not None:
            b_load(tt)
            if tt >= 3:
                b_back(tt - 3)
            if tt >= 1:
                b_tile(tt - 1)
    l1a_tail = []
    if b_tile is not None:
        b_back(NT - 3)
        b_tile(NT - 1)
        l1a_tail = [lambda: b_back(NT - 2), lambda: b_back(NT - 1)]

    if stage == 1:
        R_temp.flip()
        dbg = view(OFF_TEMP, 8192, F32)
        Bd = R_temp.new("dbg")
        for tt in range(NT):
            pt = PT[tt]
            S.dma("sp", "dbg_l", dbg[:pt, :], x1_scr[tt * 128:tt * 128 + pt, :], reads=[Bx1scr[tt]], writes=[Bd])
            dst = yp_o[tt * 128:(tt + 1) * 128, :] if tt < 8 else ys_o[:, :]
            S.dma("sp", "dbg_s", dst, dbg[:pt, :], reads=[Bd], final=True)
        S.build()
        return nc

    if stage == 6:
        S.build()
        return nc
    R_temp.flip()
    stgf = [view(OFF_HI + i * 2048, 2048, F32) for i in range(3)]
    kb16 = [view(OFF_HI + 6144 + i * 1024, 1024) for i in range(2)]
    vstg = [view(OFF_HI + 8192 + i * 1024, 1024) for i in range(3)]
    R_hi_low = Region("hi_low")
    for b in Bxc + Bx1s:
        for k, v in b.alldeps().items():
            if R_hi_low.hist.get(k, 0) < v:
                R_hi_low.hist[k] = v
    Bstgf = [R_hi_low.new("stgf%d" % i) for i in range(3)]
    Bkb16 = [R_hi_low.new("kb16_%d" % i) for i in range(2)]
    Bvstg = [R_hi_low.new("vstg%d" % i) for i in range(3)]
    ktst = [view(OFF_TEMP + i * 8192, 8192, BF16, "p (h s) -> p h s", h=4) for i in range(2)]
    Bktst = [R_temp.new("ktst%d" % i) for i in range(2)]
    vs_new = view(OFF_TEMP + 16384, 4096)
    kts_new = view(OFF_TEMP + 20480, 1024, BF16, "p (h s) -> p h s", h=16)
    Bvsn = R_temp.new("vs_new")
    Bktsn = R_temp.new("kts_new")
    Bktsend = [Buf("kt_send%d" % i) for i in range(4)]
    Bvsend = [Buf("v_send%d" % i) for i in range(4)]
    Bktall = [Buf("kt_all%d" % i) for i in range(4)]
    Bvall = [Buf("v_all%d" % i) for i in range(4)]
    RG = [[0, 1], [2, 3], [4, 5], [6, 7]]
    Bccser = Buf("cc_serial")
    bstate = {"cnt": 0}
    pending_tr = []

    def emit_kv(kv, hc):
        cnt = bstate["cnt"]
        if True:
            s = wnext()
            wv = wslot(s, "p (a n) -> p a n", a=16)
            for tt in range(NT):
                pt = PT[tt]
                if l1a_tail and tt >= 2:
                    l1a_tail.pop(0)()
                pb = 2 + (cnt % 4)
                po = bank(pb)

                def mm(e, wv=wv, po=po, tt=tt, pt=pt):
                    ins = None
                    for dc in range(16):
                        ins = e.matmul(po[:pt, :], h1T[:, dc, tt * 128:tt * 128 + pt], wv[:, dc, :],
                                       start=(dc == 0), stop=(dc == 15))
                    return ins
                S.op("pe", mm, reads=[Bh1T[tt], WB[s]], writes=[PSB[pb]])
                sf = stgf[cnt % 3]
                S.op("act", lambda e, sf=sf, po=po, pt=pt: e.copy(sf[:pt, :], po[:pt, :]), reads=[PSB[pb]],
                     writes=[Bstgf[cnt % 3]])
                if tt < 8:
                    odst = (kp_o if kv == 0 else vp_o)[tt * 128:(tt + 1) * 128, hc * 512:(hc + 1) * 512]
                else:
                    odst = (ks_o if kv == 0 else vs_o)[:, hc * 512:(hc + 1) * 512]
                S.dma("sp", "stgf%d" % (cnt % 3), odst, sf[:pt, :], reads=[Bstgf[cnt % 3]], final=True)
                if kv == 1:
                    if tt < 8:
                        vg = vstg[cnt % 3]
                        S.op("dve", lambda e, vg=vg, sf=sf, pt=pt: e.tensor_copy(vg[:pt, :], sf[:pt, :]), reads=[Bstgf[cnt % 3]],
                             writes=[Bvstg[cnt % 3]])
                        S.dma("sp", "vstg%d" % (cnt % 3), v_send[hc].ap()[tt * 128:(tt + 1) * 128, :],
                              vg[:pt, :], reads=[Bvstg[cnt % 3]], writes=[Bvsend[hc]])
                    else:
                        S.op("dve", lambda e, sf=sf, pt=pt, hc=hc: e.tensor_copy(vs_new[:pt, hc * 512:(hc + 1) * 512], sf[:pt, :]),
                             reads=[Bstgf[cnt % 3]], writes=[Bvsn])
                else:
                    kb = kb16[cnt % 2]
                    S.op("dve", lambda e, kb=kb, sf=sf, pt=pt: e.tensor_copy(kb[:pt, :], sf[:pt, :]), reads=[Bstgf[cnt % 3]],
                         writes=[Bkb16[cnt % 2]])
                    def do_tr(kb=kb, pt=pt, tt=tt, hc=hc, cnt=cnt):
                        pt_b = cnt % 2
                        pv = bank(pt_b).bitcast(BF16).rearrange("p (a t) -> p a t", a=8)

                        def trk(e, kb=kb, pv=pv, pt=pt):
                            ins = None
                            for hq in range(4):
                                ins = e.transpose(pv[:, hq, :pt], kb[:pt, hq * 128:(hq + 1) * 128], ident[:pt, :pt])
                            return ins
                        S.op("pe", trk, reads=[Bkb16[cnt % 2], Bconst], writes=[PSB[pt_b]])
                        if tt < 8:
                            kd = ktst[hc % 2][:, :, tt * 128:(tt + 1) * 128]
                            S.op("act", lambda e, kd=kd, pv=pv: e.copy(kd, pv[:, 0:4, :]), reads=[PSB[pt_b]],
                                 writes=[Bktst[hc % 2]])
                        else:
                            kd = kts_new[:, hc * 4:(hc + 1) * 4, :]
                            S.op("act", lambda e, kd=kd, pv=pv: e.copy(kd, pv[:, 0:4, 0:32]), reads=[PSB[pt_b]],
                                 writes=[Bktsn])
                    if pending_tr:
                        pending_tr.pop()()
                    pending_tr.append(do_tr)
                cnt += 1
            if pending_tr:
                pending_tr.pop()()
            if kv == 0:
                S.dma("sp", "ktst%d" % (hc % 2), kt_send[hc].ap().rearrange("(h p) s -> p h s", p=128), ktst[hc % 2],
                      reads=[Bktst[hc % 2]], writes=[Bktsend[hc]])
                if stage != 7:
                    S.custom("pool", "cck%d" % hc, 1, lambda e, hc=hc: e.collective_compute(
                        "AllGather", ALU.bypass, replica_groups=RG,
                        ins=[kt_send[hc].ap().opt()], outs=[kt_all[hc].ap().opt()]), reads=[Bktsend[hc]], writes=[Bktall[hc], Bccser])
            else:
                if stage != 7:
                    S.custom("pool", "ccv%d" % hc, 1, lambda e, hc=hc: e.collective_compute(
                        "AllGather", ALU.bypass, replica_groups=RG,
                        ins=[v_send[hc].ap().opt()], outs=[v_all[hc].ap().opt()]), reads=[Bvsend[hc]], writes=[Bvall[hc], Bccser])
            bstate["cnt"] = cnt

    R_vn.flip()
    QT = view(OFF_VN, 33792, BF16, "p (h t) -> p h t", h=16)
    sZT = view(OFF_VN + 33792, 33792, BF16, "p (h t) -> p h t", h=16)
    BQT = [R_vn.new("QT%d" % h) for h in range(16)]
    BsZ = [R_vn.new("sZ%d" % h) for h in range(16)]
    QT3 = QT.rearrange("p h (r n) -> p h r n", r=3)
    sZT3 = sZT.rearrange("p h (r n) -> p h r n", r=3)
    def emit_qz(j):
        s = wnext()
        wv = wslot(s, "p (a n) -> p a n", a=16)
        for q in range(2):
            h = 2 * j + q
            for (which, col0, pst, bst) in ((0, q * 128, psU, [PSB[0], PSB[1], PSB[2]]),
                                            (1, 256 + q * 128, psZ, [PSB[3], PSB[4], PSB[5]])):
                def mmqz(e, wv=wv, col0=col0, pst=pst):
                    ins = None
                    for r, (n0, n) in enumerate(NR):
                        o = pst[:, r * 512:r * 512 + n]
                        for dc in range(16):
                            ins = e.matmul(o, wv[:, dc, col0:col0 + 128], h1T[:, dc, n0:n0 + n],
                                           start=(dc == 0), stop=(dc == 15))
                    return ins
                S.op("pe", mmqz, reads=Bh1T + [WB[s]], writes=bst)
            S.op("dve", lambda e, h=h: e.tensor_copy(QT3[:, h, :, :], psU3), reads=[PSB[0], PSB[1], PSB[2]], writes=[BQT[h]])
            S.op("act", lambda e, h=h: e.activation(sZT3[:, h, :, :], psZ3, AF.Silu), reads=[PSB[3], PSB[4], PSB[5]],
                 writes=[BsZ[h]])

    for (kind, idx) in L1_ORDER:
        if kind == "k":
            emit_kv(0, idx)
        elif kind == "v":
            emit_kv(1, idx)
        else:
            emit_qz(idx)

    R_hi.flip()
    for rg in (R_hi_low,):
        rg.flip()
        for k, v in rg.hist.items():
            if R_hi.hist.get(k, 0) < v:
                R_hi.hist[k] = v
    R_temp.flip()
    A0 = OFF_HI
    KTp = [view(A0 + i * 8192, 8192, BF16, "p (q r s) -> p q r s", q=2, r=2) for i in range(2)]
    Vp = [view(A0 + 16384 + i * 8192, 8192, BF16, "p (r k) -> p r k", r=16) for i in range(2)]
    BKTp = [R_hi.new("KTp%d" % i) for i in range(2)]
    BVp = [R_hi.new("Vp%d" % i) for i in range(2)]
    set_off = {(0, 0): A0 + 32768, (0, 1): A0 + 38912, (1, 0): OFF_TEMP, (1, 1): OFF_TEMP + 6144}
    e_t, ec_t, sp_t, w_t, Be, Bec, Bsp, Bw = {}, {}, {}, {}, {}, {}, {}, {}
    for key, off in set_off.items():
        reg = R_hi if key[0] == 0 else R_temp
        e_t[key] = view(off, 2048, F32)
        ec_t[key] = view(off + 2048, 2048, F32)
        sp_t[key] = view(off + 4096, 1024)
        w_t[key] = view(off + 5120, 1024)
        Be[key], Bec[key], Bsp[key], Bw[key] = (reg.new("e%s" % (key,)), reg.new("ec%s" % (key,)),
                                                reg.new("sp%s" % (key,)), reg.new("w%s" % (key,)))
    fac = {(s, j): view(A0 + 45056 + (2 * s + j) * 1024, 1024) for s in range(2) for j in range(2)}
    Bfac = {key: R_hi.new("fac%s" % (key,)) for key in fac}
    se_t = view(OFF_TEMP + 12288, 1152, F32)
    sec_t = view(OFF_TEMP + 13440, 1152, F32)
    ssp_t = view(OFF_TEMP + 14592, 576)
    sw_t = view(OFF_TEMP + 15168, 576)
    ckst = view(OFF_TEMP + 21504, 2048, BF16, "p (b k) -> p b k", b=8)
    cKT = [view(OFF_TEMP + 23552 + i * 2048, 2048) for i in range(2)]
    cVh = [view(OFF_TEMP + 27648 + i * 2048, 2048, BF16, "p (b k) -> p b k", b=8) for i in range(2)]
    Bckst = R_temp.new("ckst")
    BcKT = [R_temp.new("cKT%d" % i) for i in range(2)]
    BcVh = [R_temp.new("cVh%d" % i) for i in range(2)]
    Bse, Bsec, Bssp, Bsw = R_temp.new("se"), R_temp.new("sec"), R_temp.new("ssp"), R_temp.new("sw")

    def kv_load(pi):
        i = pi % 2
        hc, hq0 = pi // 2, (2 * pi) % 4
        for q in range(2):
            S.dma("sp", "ktp%d" % i, KTp[i][:, q, :, :],
                  kt_all[hc].ap().rearrange("(r q d) s -> d q r s", r=2, q=4)[:, hq0 + q, :, :],
                  reads=[Bktall[hc]], writes=[BKTp[i]])
        S.dma("sp", "vp%d" % i, Vp[i],
              v_all[hc].ap().rearrange("(r s) c -> s r c", s=128)[:, :, hq0 * 128:hq0 * 128 + 256],
              reads=[Bvall[hc]], writes=[BVp[i]])

    def ck_load(h):
        S.dma("pool", "ckst", ckst, ck[:, h * 128:(h + 1) * 128].rearrange("(b s) k -> s b k", s=128), writes=[Bckst])

    def cv_load(h):
        i = h % 2
        S.dma("pool", "cvh%d" % i, cVh[i], cv[:, h * 128:(h + 1) * 128].rearrange("(b s) k -> s b k", s=128),
              writes=[BcVh[i]])

    TILES = []
    for gq in range(2):
        kb_max = 8 * gq + 7
        for kb in range(kb_max, -1, -1):
            i_min = max(4 * gq, kb // 2)
            c0, c1 = i_min * 128, (4 * gq + 4) * 128
            ib, rb = kb // 2, kb % 2
            TILES.append(dict(gq=gq, kb=kb, c0=c0, c1=c1, N=c1 - c0, r0=c0 - gq * 512, masked=(kb // 2) >= 4 * gq,
                              ib=ib, rb=rb, p_own=(ib + rb) % 2, first=(kb == kb_max), last=(kb == 0)))
    NTL = len(TILES)

    def sample_head(h):
        i = h % 2
        pv = bank(7).bitcast(BF16)

        def trc(e):
            ins = None
            for b in range(8):
                ins = e.transpose(pv[:, b * 128:(b + 1) * 128], ckst[:, b, :], ident)
            return ins
        S.op("pe", trc, reads=[Bckst, Bconst], writes=[PSB[7]])
        yield
        S.op("act", lambda e: e.copy(cKT[i], pv), reads=[PSB[7]], writes=[BcKT[i]])
        if h + 1 < 16:
            ck_load(h + 1)
        yield
        pL = bank(6)
        qs = QT[:, h, 1024:1056]

        def mml(e):
            ins = None
            for b in range(8):
                ins = e.matmul(pL[:, b * 32:(b + 1) * 32], cKT[i][:, b * 128:(b + 1) * 128], qs, start=True, stop=True)
            ins = e.matmul(pL[0:32, 256:288], kts_new[:, h, :], qs, start=True, stop=True)
            return ins
        S.op("pe", mml, reads=[BcKT[i], Bktsn, BQT[h]], writes=[PSB[6]])
        yield
        S.op("act", lambda e: e.activation(se_t[:, 0:256], pL[:, 0:256], AF.Exp, scale=SCALE), reads=[PSB[6]], writes=[Bse])
        S.op("act", lambda e: e.activation(se_t[0:32, 256:288], pL[0:32, 256:288], AF.Exp, scale=SCALE),
             reads=[PSB[6], Bse], writes=[Bse])
        yield
        S.op("dve", lambda e: e.tensor_tensor(se_t[0:32, 256:288], se_t[0:32, 256:288], masks[0:32, 4, 0:32], ALU.mult),
             reads=[Bse, Bmask], writes=[Bse])
        yield
        S.op("act", lambda e: e.activation(ssp_t[:, 0:256], se_t[:, 0:256], AF.Ln, bias=1.0), reads=[Bse], writes=[Bssp])
        S.op("act", lambda e: e.activation(ssp_t[0:32, 256:288], se_t[0:32, 256:288], AF.Ln, bias=1.0),
             reads=[Bse, Bssp], writes=[Bssp])
        yield

        def mmc(e):
            pc = bank(7)
            ins = None
            for b in range(8):
                o = pc[:, b * 32:(b + 1) * 32]
                ins = e.matmul(o, negU, ssp_t[:, b * 32:(b + 1) * 32], start=True, stop=False)
                for b2 in range(b + 1, 8):
                    ins = e.matmul(o, negOnes, ssp_t[:, b2 * 32:(b2 + 1) * 32], start=False, stop=False)
                ins = e.matmul(o, negOnes[0:32, :], ssp_t[0:32, 256:288], start=False, stop=True)
            ins = e.matmul(pc[0:32, 256:288], negU[0:32, 0:32], ssp_t[0:32, 256:288], start=True, stop=True)
            return ins
        S.op("pe", mmc, reads=[Bssp, Bconst, BcKT[i]], writes=[PSB[7]])
        yield
        pc = bank(7)
        S.op("act", lambda e: e.activation(sec_t[:, 0:256], pc[:, 0:256], AF.Exp), reads=[PSB[7]], writes=[Bsec])
        S.op("act", lambda e: e.activation(sec_t[0:32, 256:288], pc[0:32, 256:288], AF.Exp), reads=[PSB[7], Bsec], writes=[Bsec])
        yield
        S.op("dve", lambda e: e.tensor_tensor(sw_t[:, 0:256], se_t[:, 0:256], sec_t[:, 0:256], ALU.mult),
             reads=[Bse, Bsec], writes=[Bsw])
        S.op("dve", lambda e: e.tensor_tensor(sw_t[0:32, 256:288], se_t[0:32, 256:288], sec_t[0:32, 256:288], ALU.mult),
             reads=[Bse, Bsec, Bsw], writes=[Bsw])
        yield
        pO = bank(6)

        def mmo(e):
            ins = None
            for b in range(8):
                ins = e.matmul(pO[:, 320:352], cVh[i][:, b, :], sw_t[:, b * 32:(b + 1) * 32], start=(b == 0), stop=False)
            ins = e.matmul(pO[:, 320:352], vs_new[0:32, h * 128:(h + 1) * 128], sw_t[0:32, 256:288], start=False, stop=True)
            return ins
        S.op("pe", mmo, reads=[BcVh[i], Bvsn, Bsw], writes=[PSB[6]])
        if h + 2 < 16:
            cv_load(h + 2)
        yield
        S.op("dve", lambda e: e.tensor_tensor(sZT[:, h, 1024:1056], pO[:, 320:352], sZT[:, h, 1024:1056], ALU.mult),
             reads=[PSB[6], BsZ[h]], writes=[BsZ[h]])
        yield

    def emit_pair(pi):
        kvb = pi % 2
        heads = (2 * pi, 2 * pi + 1)
        fcur = {0: 0, 1: 0}

        def gen_samples():
            for h in heads:
                for _ in sample_head(h):
                    yield
        sg = gen_samples()

        def st_L(s, t):
            T_ = TILES[t]
            pL = bank(s)
            S.op("pe", lambda e, pL=pL, T_=T_, s=s: e.matmul(
                pL[:, 0:T_["N"]], KTp[kvb][:, s, T_["p_own"], T_["ib"] * 128:(T_["ib"] + 1) * 128],
                QT[:, heads[s], T_["c0"]:T_["c1"]], start=True, stop=True),
                reads=[BKTp[kvb], BQT[heads[s]]], writes=[PSB[s]])

        def st_ExpL(s, t):
            T_ = TILES[t]
            key = (s, t % 2)
            S.op("act", lambda e, key=key, T_=T_, s=s: e.activation(e_t[key][:, 0:T_["N"]], bank(s)[:, 0:T_["N"]], AF.Exp,
                                                                   scale=SCALE), reads=[PSB[s]], writes=[Be[key]])

        def st_mask(s, t):
            T_ = TILES[t]
            if not T_["masked"]:
                return
            key = (s, t % 2)
            mi = (T_["ib"] % 2) * 2 + T_["rb"]
            S.op("dve", lambda e, key=key, mi=mi: e.tensor_tensor(e_t[key][:, 0:128], e_t[key][:, 0:128], masks[:, mi, :],
                                                                 ALU.mult), reads=[Be[key], Bmask], writes=[Be[key]])

        def st_Ln(s, t):
            T_ = TILES[t]
            key = (s, t % 2)
            if T_["first"]:
                S.op("dve", lambda e, s=s: e.memset(fac[(s, 0)], 0.0), writes=[Bfac[(s, 0)]])
                S.op("dve", lambda e, s=s: e.memset(fac[(s, 1)], 0.0), writes=[Bfac[(s, 1)]])
                fcur[s] = 0
            S.op("act", lambda e, key=key, T_=T_: e.activation(sp_t[key][:, 0:T_["N"]], e_t[key][:, 0:T_["N"]], AF.Ln, bias=1.0),
                 reads=[Be[key]], writes=[Bsp[key]])

        def st_C(s, t):
            T_ = TILES[t]
            key = (s, t % 2)
            fc = fcur[s]
            pC = bank(2 + s)

            def mmc(e, pC=pC, key=key, T_=T_, fc=fc, s=s):
                N, r0 = T_["N"], T_["r0"]
                ins = e.matmul(pC[:, 0:N], negU, sp_t[key][:, 0:N], start=True, stop=T_["first"])
                if not T_["first"]:
                    ins = e.matmul(pC[:, 0:N], negOnes, fac[(s, fc)][:, r0:r0 + N], start=False, stop=True)
                return ins
            S.op("pe", mmc, reads=[Bsp[key], Bfac[(s, fc)], Bconst], writes=[PSB[2 + s]])

        def st_fac(s, t):
            T_ = TILES[t]
            if T_["last"]:
                return
            key = (s, t % 2)
            fc = fcur[s]
            N, r0 = T_["N"], T_["r0"]
            S.op("dve", lambda e, key=key, fc=fc, N=N, r0=r0, s=s: e.tensor_tensor(
                fac[(s, 1 - fc)][:, r0:r0 + N], fac[(s, fc)][:, r0:r0 + N], sp_t[key][:, 0:N], ALU.add),
                reads=[Bfac[(s, fc)], Bsp[key]], writes=[Bfac[(s, 1 - fc)]])
            fcur[s] = 1 - fc

        def st_ExpC(s, t):
            T_ = TILES[t]
            key = (s, t % 2)
            S.op("act", lambda e, key=key, T_=T_, s=s: e.activation(ec_t[key][:, 0:T_["N"]], bank(2 + s)[:, 0:T_["N"]], AF.Exp),
                 reads=[PSB[2 + s]], writes=[Bec[key]])

        def st_w(s, t):
            T_ = TILES[t]
            key = (s, t % 2)
            N = T_["N"]
            S.op("dve", lambda e, key=key, N=N: e.tensor_tensor(w_t[key][:, 0:N], e_t[key][:, 0:N], ec_t[key][:, 0:N], ALU.mult),
                 reads=[Be[key], Bec[key]], writes=[Bw[key]])

        def st_PV(s, t):
            T_ = TILES[t]
            key = (s, t % 2)
            pO = bank(4 + s)
            h = heads[s]
            if T_["first"]:
                S.op("pe", lambda e, pO=pO: e.matmul(pO, zero_row[0:1, 0:128], zero_row[0:1, :], start=True, stop=False),
                     reads=[Bconst], writes=[PSB[4 + s]])
            S.op("pe", lambda e, pO=pO, key=key, T_=T_, s=s: e.matmul(
                pO[:, T_["r0"]:T_["r0"] + T_["N"]], Vp[kvb][:, T_["p_own"] * 8 + T_["ib"], s * 128:(s + 1) * 128],
                w_t[key][:, 0:T_["N"]], start=False, stop=T_["last"], skip_group_check=True),
                reads=[BVp[kvb], Bw[key]], writes=[PSB[4 + s]])
            if T_["last"]:
                gq = T_["gq"]
                S.op("dve", lambda e, pO=pO, gq=gq, h=h: e.tensor_tensor(
                    sZT[:, h, gq * 512:(gq + 1) * 512], pO, sZT[:, h, gq * 512:(gq + 1) * 512], ALU.mult),
                    reads=[PSB[4 + s], BsZ[h]], writes=[BsZ[h]])

        def ok(t):
            return 0 <= t < NTL
        for k in range(-3, NTL):
            for s in range(2):
                if ok(k + 1):
                    st_Ln(s, k + 1)
            for s in range(2):
                if ok(k):
                    st_PV(s, k)
            for s in range(2):
                if ok(k + 1):
                    st_C(s, k + 1)
            for s in range(2):
                if ok(k + 1):
                    st_fac(s, k + 1)
            for s in range(2):
                if ok(k + 2):
                    st_ExpL(s, k + 2)
            for s in range(2):
                if ok(k + 2):
                    st_mask(s, k + 2)
            for s in range(2):
                if ok(k + 3):
                    st_L(s, k + 3)
            for s in range(2):
                if ok(k + 1):
                    st_ExpC(s, k + 1)
            for s in range(2):
                if ok(k + 1):
                    st_w(s, k + 1)
            next(sg, None)
        for _ in sg:
            pass

    ck_load(0)
    cv_load(0)
    cv_load(1)
    kv_load(0)
    kv_load(1)
    for pi in range(8):
        emit_pair(pi)
        if pi + 2 < 8:
            kv_load(pi + 2)


    if stage == 5:
        S.build()
        return nc
    R_temp.flip()
    R_hi.flip()
    for k, v in R_temp.hist.items():
        if R_hi.hist.get(k, 0) < v:
            R_hi.hist[k] = v
    x2 = view(OFF_HI, 73728, F32, "p (t d) -> p t d", t=9)
    Bx2 = [R_hi.new("x2_%d" % i) for i in range(NT)]
    for b in BQT:
        for k, v in b.alldeps().items():
            if R_vn.hist.get(k, 0) < v:
                R_vn.hist[k] = v
    x1c = [view(OFF_VN + i * 2048, 2048, F32) for i in range(4)]
    gbc2 = view(OFF_VN + 8192, 8192, F32)
    ystg = [view(OFF_VN + 16384 + i * 8192, 8192, F32) for i in range(2)]
    Bx1c = [Buf("x1c%d" % i, R_vn.hist) for i in range(4)]
    Bgbc2 = Buf("gbc2", R_vn.hist)
    Bystg = [Buf("ystg%d" % i, R_vn.hist) for i in range(2)]
    S.dma("sp", "gbc", gbc2, final_g.broadcast_to([128, D]), writes=[Bgbc2])
    items = [(dk, tt) for dk in range(4) for tt in range(NT)]

    def x1c_load(i):
        dk, tt = items[i]
        pt = PT[tt]
        S.dma("sp", "xc%d" % (i % 4), x1c[i % 4][:pt, :], x1_scr[tt * 128:tt * 128 + pt, dk * 512:(dk + 1) * 512],
              reads=[Bx1scr[tt]], writes=[Bx1c[i % 4]])
    x1c_load(0)
    x1c_load(1)
    for i, (dk, tt) in enumerate(items):
        pt = PT[tt]
        if tt == 0:
            cur_s = wnext()
        wv = wslot(cur_s, "p (a n) -> p a n", a=16)
        if i + 2 < len(items):
            x1c_load(i + 2)
        pb = i % 4
        po = bank(pb)

        def mmo(e, wv=wv, po=po, tt=tt, pt=pt):
            ins = None
            for hh in range(16):
                ins = e.matmul(po[:pt, :], sZT[:, hh, tt * 128:tt * 128 + pt], wv[:, hh, :],
                               start=(hh == 0), stop=(hh == 15))
            return ins
        S.op("pe", mmo, reads=BsZ + [WB[cur_s]], writes=[PSB[pb]])
        xd = x2[:pt, tt, dk * 512:(dk + 1) * 512]
        S.op("dve", lambda e, xd=xd, po=po, i=i, pt=pt: e.tensor_tensor(xd, po[:pt, :], x1c[i % 4][:pt, :], ALU.add),
             reads=[PSB[pb], Bx1c[i % 4]], writes=[Bx2[tt]])
        if dk == 3:
            ys = ystg[tt % 2]
            S.op("act", lambda e, ys=ys, tt=tt, pt=pt: e.activation(ys[:pt, :], x2[:pt, tt, :], AF.Square,
                                                                   accum_out=ssq[:pt, 18 + tt:19 + tt]),
                 reads=[Bx2[tt], Bconst], writes=[Bystg[tt % 2], Bstat[tt]])
            c = 18 + tt
            S.op("act", lambda e, c=c, pt=pt: e.activation(rsd[:pt, c:c + 1], ssq[:pt, c:c + 1], AF.Sqrt, bias=1e-6,
                                                           scale=1.0 / D), reads=[Bstat[tt]], writes=[Bstat[tt]])
            S.op("dve", lambda e, c=c, pt=pt: e.reciprocal(rsd[:pt, c:c + 1], rsd[:pt, c:c + 1]),
                 reads=[Bstat[tt]], writes=[Bstat[tt]])
            S.op("dve", lambda e, ys=ys, tt=tt, c=c, pt=pt: e.scalar_tensor_tensor(
                ys[:pt, :], x2[:pt, tt, :], rsd[:pt, c:c + 1], gbc2[:pt, :], ALU.mult, ALU.mult),
                reads=[Bx2[tt], Bstat[tt], Bgbc2], writes=[Bystg[tt % 2]])
            dst = yp_o[tt * 128:(tt + 1) * 128, :] if tt < 8 else ys_o[:, :]
            S.dma("sp", "ystg%d" % (tt % 2), dst, ys[:pt, :], reads=[Bystg[tt % 2]], final=True)

    S.build()
    return nc


_PROG = {}


def _masks_for(p):
    s = np.arange(128)[:, None]
    t = np.arange(128)[None, :]
    tri = (s < t).astype(np.float32)
    ones = np.ones((128, 128), np.float32)
    zeros = np.zeros((128, 128), np.float32)
    m = np.zeros((5, 128, 128), np.float32)
    for ipar in range(2):
        is_E = (G_BLOCKS[p][ipar] == 2 * ipar)
        m[ipar * 2 + 0] = tri if is_E else ones
        m[ipar * 2 + 1] = zeros if is_E else tri
    m[4] = tri
    return m


def kernel(x_prompt, x_sample, cache_sb_k, cache_sb_v, norm_g, final_norm_g,
           gm_w_in, gm_ln_g, gm_ln_b, gm_w_s, gm_b_s, gm_w_out, sb_w_in, sb_w_out, _stage=2):
    f = lambda a: np.ascontiguousarray(np.asarray(a, dtype=np.float32))
    x_prompt, x_sample, cache_sb_k, cache_sb_v = f(x_prompt), f(x_sample), f(cache_sb_k), f(cache_sb_v)
    if _stage not in _PROG:
        _PROG[_stage] = build_program(_stage)
    nc = _PROG[_stage]
    shared = {
        "norm_g": f(norm_g), "final_g": f(final_norm_g).reshape(1, D),
        "gm_w_in": f(gm_w_in).reshape(D, 3 * GW), "gm_ln_g": f(gm_ln_g).reshape(1, GW),
        "gm_ln_b": f(gm_ln_b).reshape(1, GW), "gm_w_s": f(gm_w_s).reshape(16 * 128, 128),
        "gm_b_s": f(gm_b_s).reshape(1, 16 * 128), "gm_w_out": f(gm_w_out).reshape(GW, D),
        "sb_w_in": f(sb_w_in).reshape(D, 4 * D), "sb_w_out": f(sb_w_out).reshape(D, D),
    }
    in_maps = []
    for c in range(NCORES):
        b, p = c // 2, c % 2
        blocks = x_prompt[b].reshape(16, 128, D)[G_BLOCKS[p]].reshape(1024, D)
        m = dict(shared)
        m["xp"] = np.ascontiguousarray(blocks)
        m["xs"] = np.ascontiguousarray(x_sample[c])
        m["ck"] = np.ascontiguousarray(cache_sb_k[0, c].reshape(1024, D))
        m["cv"] = np.ascontiguousarray(cache_sb_v[0, c].reshape(1024, D))
        m["masks"] = _masks_for(p)
        in_maps.append(m)
    res = run_bass_kernel_spmd(nc, in_maps, core_ids=list(range(NCORES)))
    y_prompt = np.zeros((4, 2048, D), np.float32)
    y_sample = np.zeros((8, 32, D), np.float32)
    k_p = np.zeros((1, 4, 2048, 16, 128), np.float32)
    v_p = np.zeros((1, 4, 2048, 16, 128), np.float32)
    k_s = np.zeros((1, 8, 32, 16, 128), np.float32)
    v_s = np.zeros((1, 8, 32, 16, 128), np.float32)
    gmv = np.zeros((1, 8, 32, GW), np.float32)
    for c in range(NCORES):
        b, p = c // 2, c % 2
        r = res.results[c]
        for i, gblk in enumerate(G_BLOCKS[p]):
            sl = slice(gblk * 128, (gblk + 1) * 128)
            y_prompt[b, sl] = r["yp"][i * 128:(i + 1) * 128]
            k_p[0, b, sl] = r["kp"][i * 128:(i + 1) * 128].reshape(128, 16, 128)
            v_p[0, b, sl] = r["vp"][i * 128:(i + 1) * 128].reshape(128, 16, 128)
        y_sample[c] = r["ys"]
        k_s[0, c] = r["ks"].reshape(32, 16, 128)
        v_s[0, c] = r["vs"].reshape(32, 16, 128)
        gmv[0, c] = r["gmv"]
    return (y_prompt, y_sample, k_p, v_p, k_s, v_s, gmv)
```
